# Optimizing a Trainium2 kernel written in Bass

```python
import jax, jax.numpy as jnp
from jax import lax
import numpy as np

D_MODEL = 1024
BATCH = 8
SEQ = 2048
DEPTH = 1
DEC_BATCH = 128
DEC_SEQ = 1
PAST_LEN = 16384
PAGE_SIZE = 128

N_META = 16
MIX_W = D_MODEL
LRU_W = MIX_W // 2
LRU_BLOCKS = 8
LRU_BLK = LRU_W // LRU_BLOCKS
LRU_C = 8.0
CONV_K = 4
GLA_DV = MIX_W - LRU_W
GLA_H = 4
GLA_DVH = GLA_DV // GLA_H
GLA_DK = GLA_DV // 2
GLA_DKH = GLA_DK // GLA_H
GLA_RANK = 16
GLA_TAU = 16.0
GLA_CHUNK = 64
IN_COLS = 2 * LRU_W + 2 * GLA_DK + 2 * GLA_DV + GLA_RANK
D_FF = -(-8 * D_MODEL // (3 * 256)) * 256
DEEPNORM_ALPHA = (2.0 * DEPTH) ** 0.25
DEEPNORM_BETA = (8.0 * DEPTH) ** -0.25
LN_EPS = 1e-5
RMS_EPS = 1e-6

kernel_name = "hymba_rglru_gla_deepnorm_step"


def _layer_norm(x, g, b):
    xf = x.astype(jnp.float32)
    mu = jnp.mean(xf, -1, keepdims=True)
    var = jnp.mean(jnp.square(xf - mu), -1, keepdims=True)
    return ((xf - mu) * lax.rsqrt(var + LN_EPS) * g + b).astype(x.dtype)


def _causal_conv(u, buf, w, b):
    full = jnp.concatenate([buf.astype(u.dtype), u], axis=1)
    T = u.shape[1]
    y = b + sum(full[:, j:j + T] * w[j] for j in range(CONV_K))
    return y, full[:, -(CONV_K - 1):]


def _rg_lru(u, h0, ga_w, ga_b, gx_w, gx_b, lam):
    B, T, W = u.shape
    ub = u.reshape(B, T, LRU_BLOCKS, LRU_BLK)
    r = jax.nn.sigmoid(jnp.einsum('btnc,ncd->btnd', ub, ga_w).reshape(B, T, W) + ga_b)
    i = jax.nn.sigmoid(jnp.einsum('btnc,ncd->btnd', ub, gx_w).reshape(B, T, W) + gx_b)
    log_a = (-LRU_C * r * jax.nn.softplus(-lam)).astype(jnp.float32)
    a = jnp.exp(log_a)
    xin = jnp.sqrt(-jnp.expm1(2.0 * log_a)) * (i * u).astype(jnp.float32)

    def step(h, ax):
        a_t, x_t = ax
        h = a_t * h + x_t
        return h, h

    hT, hs = lax.scan(step, h0.astype(jnp.float32), (a.swapaxes(0, 1), xin.swapaxes(0, 1)))
    return hs.swapaxes(0, 1).astype(u.dtype), hT.astype(h0.dtype)


def _gla_chunks(q, k, v, log_a, s0, chunk):
    B, T, H, _ = q.shape
    n = T // chunk

    def to_chunks(t):
        return t.astype(jnp.float32).reshape(B, n, chunk, H, -1).transpose(1, 0, 3, 2, 4)

    mask = jnp.tril(jnp.ones((chunk, chunk), bool))[:, :, None]

    def step(S, inp):
        qc, kc, vc, gc = inp
        b = jnp.cumsum(gc, axis=2)
        diff = b[:, :, :, None, :] - b[:, :, None, :, :]
        decay = jnp.exp(jnp.where(mask, diff, -jnp.inf))
        scores = jnp.einsum('bhtd,bhsd,bhtsd->bhts', qc, kc, decay)
        o = jnp.einsum('bhts,bhsv->bhtv', scores, vc) + jnp.einsum('bhtd,bhdv->bhtv', qc * jnp.exp(b), S)
        b_last = b[:, :, -1:, :]
        S_new = jnp.exp(b_last[:, :, 0, :])[..., None] * S + jnp.einsum(
            'bhsd,bhsv->bhdv', kc * jnp.exp(b_last - b), vc)
        return S_new, o

    S_T, o = lax.scan(step, s0.astype(jnp.float32), (to_chunks(q), to_chunks(k), to_chunks(v), to_chunks(log_a)))
    o = o.transpose(1, 0, 3, 2, 4).reshape(B, T, H, -1)
    return o, S_T


def _layer(h, conv_buf, lru_h0, gla_s0, segments, p):
    (w_in, conv_w, conv_b, ga_w, ga_b, gx_w, gx_b, lam, al_w, al_b, gn_g, w_out,
     ln1_g, ln1_b, wg, wu, wd, ln2_g, ln2_b) = p
    B, T, _ = h.shape
    proj = jnp.einsum('btd,dc->btc', h, w_in)
    i0 = LRU_W
    i1 = i0 + LRU_W
    i2 = i1 + GLA_DK
    i3 = i2 + GLA_DK
    i4 = i3 + GLA_DV
    i5 = i4 + GLA_DV
    x_lru, g_lru = proj[..., :i0], proj[..., i0:i1]
    q, k, v = proj[..., i1:i2], proj[..., i2:i3], proj[..., i3:i4]
    g_gla, a_lr = proj[..., i4:i5], proj[..., i5:]

    u, conv_new = _causal_conv(x_lru, conv_buf, conv_w, conv_b)
    hs, lru_hT = _rg_lru(u, lru_h0, ga_w, ga_b, gx_w, gx_b, lam)
    y_lru = hs * jax.nn.gelu(g_lru)

    q = q.reshape(B, T, GLA_H, GLA_DKH) * (GLA_DKH ** -0.5)
    k = k.reshape(B, T, GLA_H, GLA_DKH)
    v = v.reshape(B, T, GLA_H, GLA_DVH)
    z = (jnp.einsum('btr,rk->btk', a_lr, al_w) + al_b).astype(jnp.float32)
    log_a = (jax.nn.log_sigmoid(z) / GLA_TAU).reshape(B, T, GLA_H, GLA_DKH)
    outs = []
    s = gla_s0
    start = 0
    for length, chunk in segments:
        sl = slice(start, start + length)
        o_seg, s = _gla_chunks(q[:, sl], k[:, sl], v[:, sl], log_a[:, sl], s, chunk)
        outs.append(o_seg)
        start += length
    o = jnp.concatenate(outs, axis=1)
    o = o * lax.rsqrt(jnp.mean(o * o, -1, keepdims=True) + RMS_EPS)
    o = o.reshape(B, T, GLA_DV) * gn_g
    y_gla = o.astype(h.dtype) * jax.nn.silu(g_gla)

    mix = jnp.einsum('btc,cd->btd', jnp.concatenate([y_lru, y_gla], axis=-1), w_out)
    h = _layer_norm(DEEPNORM_ALPHA * h + mix, ln1_g, ln1_b)
    ffn = jnp.einsum('btf,fd->btd', jax.nn.silu(jnp.einsum('btd,df->btf', h, wg)) * jnp.einsum('btd,df->btf', h, wu), wd)
    h = _layer_norm(DEEPNORM_ALPHA * h + ffn, ln2_g, ln2_b)
    return h, s.astype(gla_s0.dtype), lru_hT, conv_new


def setup_inputs(seed: int = 0) -> dict:
    key = jax.random.key(seed)
    ks = jax.random.split(key, 32)
    f32 = jnp.float32

    def nrm(k, shape, scale):
        return jax.random.normal(k, shape, f32) * scale

    a0 = jax.random.uniform(ks[15], (DEPTH, LRU_W), f32, 0.9, 0.999)
    s = a0 ** (1.0 / LRU_C)
    lru_lambda = jnp.log(s) - jnp.log1p(-s)
    return {
        "x_prompt": nrm(ks[0], (BATCH, SEQ, D_MODEL), 1.0),
        "x_sample": nrm(ks[1], (DEC_BATCH, DEC_SEQ, D_MODEL), 1.0),
        "state_gla": nrm(ks[2], (DEPTH, DEC_BATCH, GLA_H, GLA_DKH, GLA_DVH), 0.5),
        "state_lru": nrm(ks[3], (DEPTH, DEC_BATCH, LRU_W), 0.5),
        "state_conv": nrm(ks[4], (DEPTH, DEC_BATCH, CONV_K - 1, LRU_W), 1.0),
        "meta_tokens": nrm(ks[5], (N_META, D_MODEL), 1.0),
        "ln_in_g": 1.0 + nrm(ks[6], (D_MODEL,), 0.05),
        "ln_in_b": nrm(ks[7], (D_MODEL,), 0.01),
        "w_in": nrm(ks[8], (DEPTH, D_MODEL, IN_COLS), D_MODEL ** -0.5),
        "conv_w": nrm(ks[9], (DEPTH, CONV_K, LRU_W), CONV_K ** -0.5),
        "conv_b": nrm(ks[10], (DEPTH, LRU_W), 0.01),
        "lru_gate_a_w": nrm(ks[11], (DEPTH, LRU_BLOCKS, LRU_BLK, LRU_BLK), LRU_BLK ** -0.5),
        "lru_gate_a_b": nrm(ks[12], (DEPTH, LRU_W), 0.01),
        "lru_gate_x_w": nrm(ks[13], (DEPTH, LRU_BLOCKS, LRU_BLK, LRU_BLK), LRU_BLK ** -0.5),
        "lru_gate_x_b": nrm(ks[14], (DEPTH, LRU_W), 0.01),
        "lru_lambda": lru_lambda,
        "gla_alpha_w": nrm(ks[16], (DEPTH, GLA_RANK, GLA_DK), GLA_RANK ** -0.5),
        "gla_alpha_b": nrm(ks[17], (DEPTH, GLA_DK), 0.1),
        "gla_norm_g": 1.0 + nrm(ks[18], (DEPTH, GLA_DV), 0.05),
        "w_out": nrm(ks[19], (DEPTH, MIX_W, D_MODEL), MIX_W ** -0.5 * DEEPNORM_BETA),
        "ln1_g": 1.0 + nrm(ks[20], (DEPTH, D_MODEL), 0.05),
        "ln1_b": nrm(ks[21], (DEPTH, D_MODEL), 0.01),
        "w_ffn_gate": nrm(ks[22], (DEPTH, D_MODEL, D_FF), D_MODEL ** -0.5),
        "w_ffn_up": nrm(ks[23], (DEPTH, D_MODEL, D_FF), D_MODEL ** -0.5),
        "w_ffn_down": nrm(ks[24], (DEPTH, D_FF, D_MODEL), D_FF ** -0.5 * DEEPNORM_BETA),
        "ln2_g": 1.0 + nrm(ks[25], (DEPTH, D_MODEL), 0.05),
        "ln2_b": nrm(ks[26], (DEPTH, D_MODEL), 0.01),
    }


def reference(x_prompt, x_sample, state_gla, state_lru, state_conv, meta_tokens, ln_in_g, ln_in_b,
              w_in, conv_w, conv_b, lru_gate_a_w, lru_gate_a_b, lru_gate_x_w, lru_gate_x_b, lru_lambda,
              gla_alpha_w, gla_alpha_b, gla_norm_g, w_out, ln1_g, ln1_b, w_ffn_gate, w_ffn_up, w_ffn_down,
              ln2_g, ln2_b):
    B, T_p, _ = x_prompt.shape
    T_s = x_sample.shape[1]
    meta = jnp.broadcast_to(meta_tokens[None].astype(x_prompt.dtype), (B, N_META, D_MODEL))
    hp = _layer_norm(jnp.concatenate([meta, x_prompt], axis=1), ln_in_g, ln_in_b)
    hs = _layer_norm(x_sample, ln_in_g, ln_in_b)
    seg_p = ((N_META, N_META), (T_p, GLA_CHUNK))
    seg_s = ((T_s, T_s),)
    gla0 = jnp.zeros((B, GLA_H, GLA_DKH, GLA_DVH), hp.dtype)
    lru0 = jnp.zeros((B, LRU_W), hp.dtype)
    conv0 = jnp.zeros((B, CONV_K - 1, LRU_W), hp.dtype)
    gp, lp, cp, gs, ls, cs = [], [], [], [], [], []
    for l in range(DEPTH):
        p = (w_in[l], conv_w[l], conv_b[l], lru_gate_a_w[l], lru_gate_a_b[l], lru_gate_x_w[l],
             lru_gate_x_b[l], lru_lambda[l], gla_alpha_w[l], gla_alpha_b[l], gla_norm_g[l], w_out[l],
             ln1_g[l], ln1_b[l], w_ffn_gate[l], w_ffn_up[l], w_ffn_down[l], ln2_g[l], ln2_b[l])
        hp, g_new, l_new, c_new = _layer(hp, conv0, lru0, gla0, seg_p, p)
        gp.append(g_new)
        lp.append(l_new)
        cp.append(c_new)
        hs, g_new, l_new, c_new = _layer(hs, state_conv[l], state_lru[l], state_gla[l], seg_s, p)
        gs.append(g_new)
        ls.append(l_new)
        cs.append(c_new)
    y_prompt = hp[:, N_META:]
    return (y_prompt, hs, jnp.stack(gp), jnp.stack(lp), jnp.stack(cp), jnp.stack(gs), jnp.stack(ls), jnp.stack(cs))
```

```python
import contextlib
import numpy as np
import concourse.bass as bass
import concourse.mybir as mybir
from concourse.bass_utils import run_bass_kernel_spmd

F32 = mybir.dt.float32
BF16 = mybir.dt.bfloat16
AF = mybir.ActivationFunctionType
ALU = mybir.AluOpType

D = 1024
TX = 2048
NMETA = 16
TP = NMETA + TX
NT = 2096
SC0 = 2080
NS = 16
DFF = 2816
NF = DFF // 128
INC = 2576
ALPHA = float(2.0 ** 0.25)
LN_EPS = 1e-5
RMS_EPS = 1e-6
TILES = [(i * 128, 128) for i in range(16)] + [(2048, 48)]
NTL = [(0, 512), (512, 512), (1024, 512), (1536, 512), (2048, 48)]
FPASS = [(0, 8), (8, 7), (15, 7)]
NPV = 72


class Sched:
    ENGS = ["pe", "act", "dve", "pool", "sp"]

    def __init__(self, K=8):
        self.K = K
        self.prog = {e: [] for e in self.ENGS}
        self.tick = {e: 0 for e in self.ENGS}
        self.waited = {e: {} for e in self.ENGS}
        self.last_w = {}
        self.readers = {}
        self.pending = {e: [] for e in self.ENGS}
        self.dma_cnt = {e: 0 for e in self.ENGS}
        self.final = {}

    def sem_names(self):
        names = ["c_" + e for e in self.ENGS]
        for e in ("sp", "pool", "act"):
            names += [f"d_{e}{s}" for s in range(self.K)]
        return names

    def _deps(self, reads, writes):
        deps = []
        for r in reads:
            t = self.last_w.get(r)
            if t is not None:
                deps.append(("raw", t))
        for w in writes:
            t = self.last_w.get(w)
            if t is not None:
                deps.append(("waw", t))
            for t in self.readers.get(w, {}).values():
                deps.append(("war", t))
        return deps

    def _emit_waits(self, eng, deps, for_dma=False):
        need = {}
        for kind, t in deps:
            sname, val, teng, is_dma = t
            if teng == eng and not is_dma and not for_dma:
                if eng == "pe":
                    continue
            assert val is not None, f"unresolved dep on {sname} from {eng}"
            if need.get(sname, 0) < val:
                need[sname] = val
        for sname, val in need.items():
            if self.waited[eng].get(sname, 0) < val:
                self.waited[eng][sname] = val
                self.prog[eng].append(("wait", sname, val))

    def _update(self, tok, reads, writes):
        for r in reads:
            self.readers.setdefault(r, {})[tok[0]] = tok
        for w in writes:
            self.last_w[w] = tok
            self.readers[w] = {}

    def op(self, eng, fn, reads=(), writes=(), signal=True):
        self._emit_waits(eng, self._deps(reads, writes))
        if signal:
            self.tick[eng] += 1
            tok = ["c_" + eng, self.tick[eng], eng, False]
            for p in self.pending[eng]:
                p[1] = self.tick[eng]
            self.pending[eng] = []
        else:
            tok = ["c_" + eng, None, eng, False]
            self.pending[eng].append(tok)
        self.prog[eng].append(("op", fn, signal))
        self._update(tok, reads, writes)

    def dma(self, eng, out, in_, reads=(), writes=()):
        j = self.dma_cnt[eng]
        self.dma_cnt[eng] += 1
        s, m = j % self.K, j // self.K
        sname = f"d_{eng}{s}"
        deps = self._deps(reads, writes)
        if m > 0:
            deps.append(("raw", [sname, 16 * m, eng, True]))
        self._emit_waits(eng, deps, for_dma=True)
        tok = [sname, 16 * (m + 1), eng, True]
        self.prog[eng].append(("dma", out, in_, sname))
        self._update(tok, reads, writes)
        self.final[sname] = 16 * (m + 1)

    def barrier(self):
        for e in self.ENGS:
            assert not self.pending[e]
        for e in self.ENGS:
            for o in self.ENGS:
                if o != e and self.tick[o] > self.waited[e].get("c_" + o, 0):
                    self.waited[e]["c_" + o] = self.tick[o]
                    self.prog[e].append(("wait", "c_" + o, self.tick[o]))
            for sname, val in self.final.items():
                if self.waited[e].get(sname, 0) < val:
                    self.waited[e][sname] = val
                    self.prog[e].append(("wait", sname, val))

    def finish(self):
        for sname, val in self.final.items():
            if self.waited["sp"].get(sname, 0) < val:
                self.waited["sp"][sname] = val
                self.prog["sp"].append(("wait", sname, val))

    def emit(self, block, sems):
        def run(name, e):
            for it in self.prog[name]:
                if it[0] == "wait":
                    e.wait_ge(sems[it[1]], it[2])
                elif it[0] == "op":
                    inst = it[1](e)
                    if it[2]:
                        inst.then_inc(sems["c_" + name], 1)
                else:
                    e.dma_start(out=it[1], in_=it[2]).then_inc(sems[it[3]], 16)

        @block.sync
        def _(e):
            run("sp", e)

        @block.scalar
        def _(e):
            run("act", e)

        @block.vector
        def _(e):
            run("dve", e)

        @block.gpsimd
        def _(e):
            run("pool", e)

        @block.tensor
        def _(e):
            run("pe", e)


def build_nc():
    nc = bass.Bass("TRN2", target_bir_lowering=False)

    def din(name, shape):
        return nc.dram_tensor(name, list(shape), F32, kind="ExternalInput").ap()

    def dout(name, shape):
        return nc.dram_tensor(name, list(shape), F32, kind="ExternalOutput").ap()

    xp = din("xp", [TX, D])
    xs = din("xs", [NS, D])
    sg_in = din("sg", [NS, 4, 64, 128])
    slT_in = din("slT", [128, 4, NS])
    scT_in = din("scT", [128, 4, 3, NS])
    meta = din("meta", [NMETA, D])
    lng = [din("ln_in_g", [D]), din("ln1_g", [D]), din("ln2_g", [D])]
    lnb = [din("ln_in_b", [D]), din("ln1_b", [D]), din("ln2_b", [D])]
    w_in = din("w_in", [D, INC])
    gaw = din("gaw", [8, 64, 64])
    gxw = din("gxw", [8, 64, 64])
    alw_in = din("alw", [16, 256])
    pvec_in = din("pvec", [128, NPV])
    w_out = din("w_out", [D, D])
    wg = din("wg", [D, DFF])
    wu = din("wu", [D, DFF])
    wd = din("wd", [DFF, D])
    ident_in = din("ident", [128, 128])
    tri_in = din("tri", [128, 128])

    y_p = dout("y_p", [TX, D])
    y_s = dout("y_s", [NS, D])
    gla_p = dout("gla_p", [4, 64, 128])
    osm_out = dout("osm", [128, 4, 20])
    gla_s = dout("gla_s", [NS, 4, 64, 128])
    conv_s = dout("conv_s", [128, 4, 3, NS])

    S = Sched()
    es = contextlib.ExitStack()
    with es:
        def sb(name, shape, dt):
            return es.enter_context(nc.sbuf_tensor("s_" + name, list(shape), dt))

        big = sb("big", [128, 17 * 1024], F32)
        hT = sb("hT", [128, 8, NT], BF16)
        ymT = sb("ymT", [128, 8, NT], BF16)
        ws = sb("ws", [128, 8, 1024], BF16)
        xin = sb("xin", [128, 2, 1024], F32)
        ht = sb("ht", [128, 2, 1024], F32)
        gb = sb("gb", [128, 2, 1024], F32)
        wgu = sb("wgu", [128, 4224], BF16)
        stmp = sb("stmp", [128, 2, 512], F32)
        ident = sb("ident", [128, 128], F32)
        identb = sb("identb", [128, 128], BF16)
        tri = sb("tri", [128, 128], F32)
        ones = sb("ones", [128, 128], F32)
        pvec = sb("pvec", [128, NPV], F32)
        dvec = sb("dvec", [128, 16], F32)
        gw = sb("gw", [128, 8, 128], BF16)
        alw = sb("alw", [16, 256], F32)
        stat0 = sb("stat0", [128, 17, 2], F32)
        bst = sb("bst", [128, 2, 2, 6], F32)
        mv = sb("mv", [128, 2, 2], F32)
        sd = sb("sd", [128, 2, 2], F32)
        rn = sb("rn", [128, 2, 2], F32)
        ssum = sb("ssum", [128, 2, 4], F32)
        S32 = sb("S32", [128, 128], F32)
        Stmp = sb("Stmp", [128, 128], F32)
        Sbf = sb("Sbf", [128, 128], BF16)
        scm = sb("scm", [128, 4, 128], BF16)
        elast = sb("elast", [128, 17], F32)
        qs = sb("qs", [128, NS], BF16)
        ksb = sb("ksb", [48, 256], BF16)
        ebs = sb("ebs", [128, NS], F32)
        smallt = sb("smallt", [128, 4, NS], F32)
        osm = sb("osm", [128, 4, 20], F32)
        brow = sb("brow", [33, 2, 1024], F32)
        hlb = sb("hlb", [33, 2, 1024], BF16)
        onesb = sb("onesb", [33, 128], BF16)
        ps = es.enter_context(nc.psum_tensor("ps", [128, 8, 512], F32))
        sems = {n: es.enter_context(nc.semaphore(n)) for n in S.sem_names()}

        def bank(b):
            return ps[:, b, :]

        bank_ctr = [0]

        def nextbank():
            b = bank_ctr[0] % 8
            bank_ctr[0] += 1
            return b

        def pv(col):
            return pvec[:, col:col + 1]
        PV_CW = lambda c, j: pv(c * 8 + j)
        PV_CB = lambda c: pv(c * 8 + 4)
        PV_GAB = lambda c: pv(c * 8 + 5)
        PV_GXB = lambda c: pv(c * 8 + 6)
        PV_LAM = lambda c: pv(c * 8 + 7)
        PV_ALB = lambda hp: pv(32 + hp)
        PV_GNG = lambda h: pv(34 + h)
        DV = lambda col: dvec[:, col:col + 1]

        S.dma("sp", ident[:], ident_in, writes=["ident"])
        S.dma("sp", tri[:], tri_in, writes=["tri"])
        S.dma("sp", pvec[:], pvec_in, writes=["pvec"])
        S.dma("sp", alw[:], alw_in, writes=["alw"])
        S.dma("pool", identb[:], ident_in, writes=["identb"])
        S.op("dve", lambda e: e.memset(ones[:], 1.0), writes=["ones"])
        S.op("dve", lambda e: e.memset(onesb[:], 1.0), writes=["onesb"])
        S.op("dve", lambda e: e.memset(hlb[:], 0.0), writes=["hlb_init"])
        S.op("dve", lambda e: e.memset(dvec[:, 10:11], LN_EPS), writes=["dvc"])
        S.op("dve", lambda e: e.memset(dvec[:, 11:12], 1.0), writes=["dvc"])
        S.op("dve", lambda e: e.memset(dvec[:, 12:13], RMS_EPS), writes=["dvc"])
        S.op("dve", lambda e: e.memset(S32[:], 0.0), writes=["S32"])
        S.op("pool", lambda e: e.memset(gw[:], 0.0), writes=["gw"])
        for c in range(4):
            for g, src in enumerate((gaw, gxw)):
                for blk in range(2):
                    S.dma("pool", gw[64 * blk:64 * blk + 64, 2 * c + g, 64 * blk:64 * blk + 64],
                          src[2 * c + blk], writes=["gw"])
        for c in range(4):
            S.op("act", lambda e, c=c: e.activation(out=dvec[:, c:c + 1], in_=PV_LAM(c), func=AF.Exp, scale=-1.0),
                 reads=["pvec"], writes=["dv%d" % c])
            S.op("act", lambda e, c=c: e.activation(out=dvec[:, c:c + 1], in_=dvec[:, c:c + 1], func=AF.Ln,
                                                    bias=DV(11)),
                 reads=["dv%d" % c, "dvc"], writes=["dv%d" % c])
            S.op("dve", lambda e, c=c: e.tensor_scalar(out=dvec[:, 4 + c:5 + c], in0=dvec[:, c:c + 1],
                                                       scalar1=-16.0, scalar2=None, op0=ALU.mult),
                 reads=["dv%d" % c], writes=["dw%d" % c])
            S.op("dve", lambda e, c=c: e.tensor_scalar(out=dvec[:, c:c + 1], in0=dvec[:, c:c + 1],
                                                       scalar1=-8.0, scalar2=None, op0=ALU.mult),
                 reads=["dv%d" % c], writes=["dv%d" % c])
        S.op("dve", lambda e: e.tensor_scalar(out=dvec[:, 8:10], in0=pvec[:, 32:34], scalar1=-1.0, scalar2=None,
                                              op0=ALU.mult), reads=["pvec"], writes=["dnalb"])

        def load_gb(slot, src, scale=None):
            S.dma("sp", gb[:, slot, :], src.partition_broadcast(128), writes=["gb%d" % slot])
            if scale is not None:
                S.op("act", lambda e: e.activation(out=gb[:, slot, :], in_=gb[:, slot, :], func=AF.Copy, scale=scale),
                     reads=["gb%d" % slot], writes=["gb%d" % slot])

        def load_brow(slot, src, scale):
            bk_ = "brow%d" % slot
            for p in (0, 32):
                S.dma("sp", brow[p:p + 1, slot, :], src.rearrange("(o d) -> o d", o=1), writes=[bk_ + "_%d" % p])
                S.op("act", lambda e, p=p: e.activation(out=brow[p:p + 1, slot, :], in_=brow[p:p + 1, slot, :],
                                                        func=AF.Copy, scale=scale),
                     reads=[bk_ + "_%d" % p], writes=[bk_ + "_%d" % p])
                S.op("act", lambda e, p=p: e.activation(out=hlb[p:p + 1, slot, :], in_=brow[p:p + 1, slot, :],
                                                        func=AF.Copy),
                     reads=[bk_ + "_%d" % p, "hlb_init"], writes=[bk_])
            S.op("dve", lambda e: e.tensor_tensor(out=brow[32:33, slot, :], in0=brow[32:33, slot, :],
                                                  in1=hlb[32:33, slot, :], op=ALU.subtract),
                 reads=[bk_, bk_ + "_32"], writes=[bk_ + "_32"])
            S.op("act", lambda e: e.activation(out=hlb[32:33, slot, :], in_=brow[32:33, slot, :], func=AF.Copy),
                 reads=[bk_ + "_32"], writes=[bk_])

        def ln_stats(src, P, par, rstd_out, nmr_out, extra_reads, key):
            kb = "bst%d" % par
            S.op("dve", lambda e: e.bn_stats(out=bst[:P, par, 0, :], in_=src[:, 0:512]),
                 reads=extra_reads, writes=[kb + "a"])
            S.op("dve", lambda e: e.bn_stats(out=bst[:P, par, 1, :], in_=src[:, 512:1024]),
                 reads=extra_reads, writes=[kb + "b"])
            S.op("dve", lambda e: e.bn_aggr(out=mv[:P, par, :],
                                            in_=bst[:P, par, :, :].rearrange("p a b -> p (a b)")),
                 reads=[kb + "a", kb + "b"], writes=["mv%d" % par])
            S.op("act", lambda e: e.activation(out=sd[:P, par, 0:1], in_=mv[:P, par, 1:2], func=AF.Sqrt,
                                               bias=dvec[:P, 10:11]),
                 reads=["mv%d" % par, "dvc"], writes=["sd%d" % par])
            S.op("dve", lambda e: e.reciprocal(out=rstd_out, in_=sd[:P, par, 0:1]),
                 reads=["sd%d" % par], writes=[key + "r"])
            S.op("dve", lambda e: e.tensor_scalar(out=nmr_out, in0=mv[:P, par, 0:1], scalar1=rstd_out,
                                                  scalar2=-1.0, op0=ALU.mult, op1=ALU.mult),
                 reads=["mv%d" % par, key + "r"], writes=[key + "n"])

        def load_x_tile(i, par):
            c0, P = TILES[i]
            k = "xin%d" % par
            if i == 0:
                S.dma("sp", xin[0:16, par, :], meta, writes=[k])
                S.dma("sp", xin[16:128, par, :], xp[0:112, :], writes=[k])
            elif i < 16:
                S.dma("sp", xin[:, par, :], xp[128 * i - 16:128 * i + 112, :], writes=[k])
            else:
                S.op("pool", lambda e: e.memset(xin[0:48, par, :], 0.0), writes=[k])
                S.dma("sp", xin[0:16, par, :], xp[2032:2048, :], writes=[k])
                S.dma("sp", xin[32:48, par, :], xs, writes=[k])

        def pipeline(stages, n):
            ns = len(stages)
            for t in range(n + ns - 1):
                for s_ in range(ns - 1, -1, -1):
                    it = t - s_
                    if 0 <= it < n:
                        stages[s_](it)

        tb_ctr = [0]

        def tbank():
            b = 4 + tb_ctr[0] % 4
            tb_ctr[0] += 1
            return b

        def transpose_tile(src, P, c0, dstT, src_keys, dkey, pcol, balloc=None, act_kk=(0, 2)):
            for half in range(2):
                b = (balloc or nextbank)()
                for kk in range(4):
                    k = half * 4 + kk
                    S.op("pe", lambda e, k=k, kk=kk, b=b: e.transpose(
                        out=ps[:, b, kk * 128:kk * 128 + P], in_=src[:, k * 128:(k + 1) * 128],
                        identity=ident[:P, :P]),
                        reads=src_keys + ["ident"], writes=["bank%d" % b], signal=(kk == 3))
                for kk in range(4):
                    k = half * 4 + kk
                    srcv = ps[:, b, kk * 128:kk * 128 + P]
                    dstv = dstT[:, k, c0:c0 + P]
                    gcol, bcol = pv(pcol + k), pv(pcol + 8 + k)
                    if half == 0:
                        S.op("act", lambda e, srcv=srcv, dstv=dstv, gcol=gcol, bcol=bcol: e.activation(
                            out=dstv, in_=srcv, func=AF.Identity, scale=gcol, bias=bcol),
                            reads=["bank%d" % b, "pvec"], writes=["%s%d" % (dkey, k)])
                    else:
                        S.op("dve", lambda e, srcv=srcv, dstv=dstv, gcol=gcol, bcol=bcol: e.tensor_scalar(
                            out=dstv, in0=srcv, scalar1=gcol, scalar2=bcol, op0=ALU.mult, op1=ALU.add),
                            reads=["bank%d" % b, "pvec"], writes=["%s%d" % (dkey, k)])

        def proj_fm(wsrc, wkey, col0, M, xT, xkey, n, b):
            n0, N = NTL[n]
            for k in range(8):
                S.op("pe", lambda e, k=k: e.matmul(ps[0:M, b, 0:N], lhsT=wsrc[:, k, col0:col0 + M],
                                                   rhs=xT[:, k, n0:n0 + N], start=(k == 0), stop=(k == 7)),
                     reads=[wkey, "%s%d" % (xkey, k)], writes=["bank%d" % b], signal=(k == 7))

        S.dma("pool", ws[:], w_in.rearrange("(ko p) c -> p ko c", p=128)[:, :, 0:1024], writes=["ws"])
        S.dma("pool", wgu[:, 0:4224].rearrange("p (k c) -> p k c", c=528),
              w_in.rearrange("(ko p) c -> p ko c", p=128)[:, :, 2048:2576], writes=["wgu"])
        xbuf = [xin[:, 0, :], xin[:, 1, :], ht[:, 0, :], ht[:, 1, :]]

        def load_x_buf(i):
            c0, P = TILES[i]
            xb, xk = xbuf[i % 4], "xb%d" % (i % 4)
            if i == 0:
                S.dma("sp", xb[0:16, :], meta, writes=[xk])
                S.dma("sp", xb[16:128, :], xp[0:112, :], writes=[xk])
            elif i < 16:
                S.dma("sp", xb[:, :], xp[128 * i - 16:128 * i + 112, :], writes=[xk])
            else:
                S.op("pool", lambda e: e.memset(xb[0:48, :], 0.0), writes=[xk])
                S.dma("sp", xb[0:16, :], xp[2032:2048, :], writes=[xk])
                S.dma("sp", xb[32:48, :], xs, writes=[xk])

        def A0a(i):
            c0, P = TILES[i]
            par = i % 2
            xb, xk = xbuf[i % 4], "xb%d" % (i % 4)
            load_x_buf(i)
            kb = "bst%d" % par
            S.op("dve", lambda e: e.bn_stats(out=bst[:P, par, 0, :], in_=xb[:P, 0:512]), reads=[xk], writes=[kb + "a"])
            S.op("dve", lambda e: e.bn_stats(out=bst[:P, par, 1, :], in_=xb[:P, 512:1024]), reads=[xk],
                 writes=[kb + "b"])
            S.op("dve", lambda e: e.bn_aggr(out=mv[:P, par, :], in_=bst[:P, par, :, :].rearrange("p a b -> p (a b)")),
                 reads=[kb + "a", kb + "b"], writes=["mv%d" % par])

        def A0b_act(i):
            c0, P = TILES[i]
            par = i % 2
            S.op("act", lambda e: e.activation(out=sd[:P, par, 0:1], in_=mv[:P, par, 1:2], func=AF.Sqrt,
                                               bias=dvec[:P, 10:11]),
                 reads=["mv%d" % par, "dvc"], writes=["sd%d" % par])

        def A0b_dve(i):
            c0, P = TILES[i]
            par = i % 2
            key = "st0_%d" % i
            S.op("dve", lambda e: e.reciprocal(out=stat0[:P, i, 0:1], in_=sd[:P, par, 0:1]),
                 reads=["sd%d" % par], writes=[key + "r"])
            S.op("dve", lambda e: e.tensor_scalar(out=stat0[:P, i, 1:2], in0=mv[:P, par, 0:1],
                                                  scalar1=stat0[:P, i, 0:1], scalar2=-1.0,
                                                  op0=ALU.mult, op1=ALU.mult),
                 reads=["mv%d" % par, key + "r"], writes=[key + "n"])

        def A1(i):
            c0, P = TILES[i]
            xb, xk = xbuf[i % 4], "xb%d" % (i % 4)
            S.op("act", lambda e: e.activation(
                out=xb[:P, :], in_=xb[:P, :], func=AF.Identity,
                scale=stat0[:P, i, 0:1], bias=stat0[:P, i, 1:2]),
                reads=[xk, "st0_%dr" % i, "st0_%dn" % i], writes=[xk])

        def A2(i):
            c0, P = TILES[i]
            xb, xk = xbuf[i % 4], "xb%d" % (i % 4)
            transpose_tile(xb[:P, :], P, c0, hT, [xk], "hT", 40)

        for s_ in range(17 + 3):
            for fn, sk in ((A2, 3), (A1, 2), (A0b_act, 1), (A0a, 0), (A0b_dve, 1)):
                if 0 <= s_ - sk < 17:
                    fn(s_ - sk)

        RW = 2104
        def R(j, w=NT, off=0):
            return big[:, j * RW + off:j * RW + off + w]
        def Rb(j, half):
            v = big[:, j * RW:j * RW + 2096].bitcast(BF16)
            return v[:, half * NT:(half + 1) * NT]
        S.dma("sp", smallt[:], slT_in, writes=["slT"])
        sct = big[:, 8 * RW:8 * RW + 4 * 3 * NS].rearrange("p (c k j) -> p c k j", c=4, k=3)
        S.dma("sp", sct, scT_in, writes=["scT"])
        S.dma("sp", conv_s[:, :, 0:2, :], sct[:, :, 1:3, :], reads=["scT"])
        RA, RI, RS = 5, 6, 7
        S.op("pool", lambda e: e.memset(R(0, 3), 0.0), writes=["XL"])

        def L0(c):
            p2 = c % 2
            GG = "GG%d" % p2
            for n in range(5):
                n0, N = NTL[n]
                b = nextbank()
                proj_fm(ws, "ws", 128 * c, 128, hT, "hT", n, b)
                S.op("dve", lambda e, b=b, n0=n0, N=N: e.tensor_copy(out=R(0, N, 3 + n0), in_=ps[:, b, 0:N]),
                     reads=["bank%d" % b], writes=["XL"])
            for n in range(5):
                n0, N = NTL[n]
                b = nextbank()
                proj_fm(ws, "ws", 512 + 128 * c, 128, hT, "hT", n, b)
                S.op("act", lambda e, b=b, n0=n0, N=N: e.activation(out=Rb(1 + p2, 0)[:, n0:n0 + N],
                                                                    in_=ps[:, b, 0:N], func=AF.Gelu_apprx_tanh),
                     reads=["bank%d" % b], writes=[GG])

        def L1(c):
            p2 = c % 2
            RU = 3 + p2
            U, UB = "U%d" % p2, "UB%d" % p2
            S.op("dve", lambda e: e.tensor_scalar(out=R(RU), in0=R(0, NT, 0), scalar1=PV_CW(c, 0),
                                                  scalar2=PV_CB(c), op0=ALU.mult, op1=ALU.add),
                 reads=["XL", "pvec"], writes=[U])
            for j in range(1, 4):
                S.op("dve", lambda e, j=j: e.scalar_tensor_tensor(out=R(RU), in0=R(0, NT, j),
                                                                  scalar=PV_CW(c, j), in1=R(RU),
                                                                  op0=ALU.mult, op1=ALU.add),
                     reads=["XL", U, "pvec"], writes=[U])
            us = R(RU, NS, SC0)
            S.op("dve", lambda e: e.tensor_scalar(out=us, in0=R(0, NS, 3 + SC0), scalar1=PV_CW(c, 3),
                                                  scalar2=PV_CB(c), op0=ALU.mult, op1=ALU.add),
                 reads=["XL", U, "pvec"], writes=[U])
            for j in range(3):
                S.op("dve", lambda e, j=j: e.scalar_tensor_tensor(
                    out=us, in0=sct[:, c, j, :], scalar=PV_CW(c, j), in1=us, op0=ALU.mult, op1=ALU.add),
                    reads=["scT", U, "pvec"], writes=[U])
            S.op("dve", lambda e: e.tensor_copy(out=Rb(1 + p2, 1), in_=R(RU)), reads=[U], writes=[UB])
            S.op("act", lambda e: e.activation(out=osm[:, c, 17:20], in_=R(0, 3, 3 + TP - 3), func=AF.Copy),
                 reads=["XL"], writes=["osm"])
            S.dma("sp", conv_s[:, c, 2, :], R(0, NS, 3 + SC0), reads=["XL"])

        def L2a(c):
            p2 = c % 2
            UB = "UB%d" % p2
            for g, dst, bcol, key in ((1, RI, PV_GXB(c), "I"), (0, RA, PV_GAB(c), "A")):
                for n in range(5):
                    n0, N = NTL[n]
                    b = nextbank()
                    S.op("pe", lambda e, b=b, n0=n0, N=N, g=g: e.matmul(
                        ps[:, b, 0:N], lhsT=gw[:, 2 * c + g, :], rhs=Rb(1 + p2, 1)[:, n0:n0 + N], start=True, stop=True),
                        reads=["gw", UB], writes=["bank%d" % b])
                    S.op("act", lambda e, b=b, n0=n0, N=N, dst=dst, bcol=bcol: e.activation(
                        out=R(dst, N, n0), in_=ps[:, b, 0:N], func=AF.Sigmoid, bias=bcol),
                        reads=["bank%d" % b, "pvec"], writes=[key])
            S.op("act", lambda e: e.activation(out=R(RS), in_=R(RA), func=AF.Exp, scale=DV(4 + c)),
                 reads=["A", "dw%d" % c], writes=["SQ"])
            S.op("act", lambda e: e.activation(out=R(RA), in_=R(RA), func=AF.Exp, scale=DV(c)),
                 reads=["A", "dv%d" % c], writes=["A"])
            S.op("act", lambda e: e.activation(out=R(RS), in_=R(RS), func=AF.Sqrt, scale=-1.0, bias=DV(11)),
                 reads=["SQ", "dvc"], writes=["SQ"])

        def L2b(c):
            p2 = c % 2
            RU = 3 + p2
            U, GG = "U%d" % p2, "GG%d" % p2
            S.op("dve", lambda e: e.tensor_tensor(out=R(RI), in0=R(RI), in1=R(RU), op=ALU.mult),
                 reads=["I", U], writes=["I"])
            S.op("dve", lambda e: e.tensor_tensor(out=R(RI), in0=R(RI), in1=R(RS), op=ALU.mult),
                 reads=["I", "SQ"], writes=["I"])
            S.op("dve", lambda e: e.memset(R(RU, 16, TP), 0.0), reads=[U], writes=[U])
            S.op("dve", lambda e: e.tensor_tensor_scan(out=R(RU, TP), data0=R(RA, TP), data1=R(RI, TP), initial=0.0,
                                                       op0=ALU.mult, op1=ALU.add),
                 reads=["A", "I", U], writes=[U])
            S.op("dve", lambda e: e.tensor_tensor(out=R(RU, NS, SC0), in0=R(RA, NS, SC0), in1=smallt[:, c, :],
                                                  op=ALU.mult),
                 reads=["A", "slT", U], writes=[U])
            S.op("dve", lambda e: e.tensor_tensor(out=R(RU, NS, SC0), in0=R(RU, NS, SC0), in1=R(RI, NS, SC0),
                                                  op=ALU.add),
                 reads=["I", U], writes=[U])
            S.op("dve", lambda e: e.tensor_tensor(out=ymT[:, c, :], in0=R(RU), in1=Rb(1 + p2, 0), op=ALU.mult),
                 reads=[U, GG], writes=["ymT%d" % c])
            S.op("dve", lambda e: e.tensor_copy(out=osm[:, c, 0:16], in_=R(RU, NS, SC0)),
                 reads=[U], writes=["osm"])
            S.op("dve", lambda e: e.tensor_copy(out=osm[:, c, 16:17], in_=R(RU, 1, TP - 1)),
                 reads=[U], writes=["osm"])

        for t in range(6):
            if 0 <= t - 2 < 4:
                L2a(t - 2)
            if 0 <= t - 1 < 4:
                L1(t - 1)
            if 0 <= t - 2 < 4:
                L2b(t - 2)
            if 0 <= t < 4:
                L0(t)
                if t == 3:
                    S.dma("pool", ws[:], w_in.rearrange("(ko p) c -> p ko c", p=128)[:, :, 1024:2048], writes=["ws"])
        S.dma("sp", osm_out, osm[:], reads=["osm"])
        O_VT, O_X, O_QK, O_SG, O_KT, O_BL, O_S0, O_KSV, O_SB, O_V16 = 0, 2104, 4208, 6312, 8416, 9504, 11680, 13728, 14496, 15584
        bl = big[:, O_BL:O_BL + NT]
        qkv = big[:, O_QK:O_QK + 2096].bitcast(BF16)
        qT_ = qkv[:, 0:NT]
        kT_ = qkv[:, NT:2 * NT]
        ktok = big[:, O_KT:O_KT + 1088].bitcast(BF16).rearrange("p (i c) -> p i c", c=128)
        vtok16 = big[:, O_VT:O_VT + 2048].bitcast(BF16).rearrange("p (i c) -> p i c", c=256)
        vt16 = big[0:48, O_V16:O_V16 + 128].bitcast(BF16)

        def vt(i, rows):
            return vtok16[0:rows, i, :] if i < 16 else vt16[0:rows, :]

        def vproj(hp, tiles, extra):
            for i in tiles:
                c0, P = TILES[i]
                b = nextbank()
                for k in range(8):
                    S.op("pe", lambda e, k=k, b=b, c0=c0, P=P: e.matmul(
                        ps[0:P, b, 0:256], lhsT=hT[:, k, c0:c0 + P], rhs=ws[:, k, 512 + 256 * hp:768 + 256 * hp],
                        start=(k == 0), stop=(k == 7)),
                        reads=["ws", "hT%d" % k], writes=["bank%d" % b], signal=(k == 7))
                S.op("act", lambda e, b=b, i=i, P=P: e.activation(out=vt(i, P), in_=ps[0:P, b, 0:256], func=AF.Copy),
                     reads=["bank%d" % b], writes=["vtok"] + extra)
        sgv = big[:, O_SG:O_SG + 2096].bitcast(BF16).rearrange("p (h c) -> p h c", c=NT)
        S0s = [big[:, O_S0:O_S0 + 2048].rearrange("p (j v) -> p j v", v=128),
               gb[:].rearrange("p a b -> p (a b)").rearrange("p (j v) -> p j v", v=128)]
        S0keys = [["S0_0"], ["gb0", "gb1"]]
        aT = big[0:16, O_X:O_X + NT]
        Vexp = big[32:48, O_X:O_X + 1024].bitcast(BF16).rearrange("p (j v) -> p j v", v=128)
        S0b = big[:, 15712:15712 + 1024].bitcast(BF16).rearrange("p (j v) -> p j v", v=128)
        ksv = big[0:48, O_KSV:O_KSV + 768]
        SbA = big[:, O_SB:O_SB + 1088].bitcast(BF16).rearrange("p (i c) -> p i c", c=128)
        WB = ws
        WC = wgu[:, 0:4224].rearrange("p (k c) -> p k c", c=528)
        w_in_r = w_in.rearrange("(ko p) c -> p ko c", p=128)
        for n in range(5):
            n0, N = NTL[n]
            b = nextbank()
            proj_fm(WC, "wgu", 512, 16, hT, "hT", n, b)
            S.op("act", lambda e, b=b, n0=n0, N=N: e.activation(out=aT[:, n0:n0 + N], in_=ps[0:16, b, 0:N],
                                                                func=AF.Copy),
                 reads=["bank%d" % b], writes=["aT", "GG0", "UB0"])

        def gproj(hp, extra):
            for hl in range(2):
                h = 2 * hp + hl
                for n in range(5):
                    n0, N = NTL[n]
                    b = nextbank()
                    proj_fm(WC, "wgu", 128 * h, 128, hT, "hT", n, b)
                    S.op("act", lambda e, b=b, hl=hl, n0=n0, N=N: e.activation(
                        out=sgv[:, hl, n0:n0 + N], in_=ps[:, b, 0:N], func=AF.Silu),
                        reads=["bank%d" % b], writes=["sg"] + extra)

        gproj(0, ["U0"])
        load_brow(0, lnb[0], ALPHA)
        vproj(0, range(16), ["XL"])
        S.barrier()
        for hp in range(2):
            for hl in range(2):
                S.dma("sp", S0s[hp][64 * hl:64 * hl + 64, :, :],
                      sg_in.rearrange("j h d v -> h d j v")[2 * hp + hl], writes=S0keys[hp])
        for hp in range(2):
            S0 = S0s[hp]
            S0k = S0keys[hp]
            if hp == 1:
                gproj(1, [])
            for n in range(5):
                n0, N = NTL[n]
                b = nextbank()
                S.op("pe", lambda e, b=b, n0=n0, N=N, hp=hp: e.matmul(
                    ps[:, b, 0:N], lhsT=alw[0:16, 128 * hp:128 * hp + 128], rhs=aT[:, n0:n0 + N],
                    start=True, stop=True), reads=["alw", "aT"], writes=["bank%d" % b])
                S.op("act", lambda e, b=b, N=N, hp=hp: e.activation(out=stmp[:, 0, 0:N], in_=ps[:, b, 0:N],
                                                                    func=AF.Exp, scale=-1.0, bias=DV(8 + hp)),
                     reads=["bank%d" % b, "dnalb"], writes=["stmp0"])
                S.op("act", lambda e, N=N: e.activation(out=stmp[:, 1, 0:N], in_=stmp[:, 0, 0:N], func=AF.Ln,
                                                        bias=DV(11)),
                     reads=["stmp0", "dvc"], writes=["stmp1"])
                if n < 4:
                    for q in range(4):
                        S.op("dve", lambda e, q=q, n0=n0: e.tensor_tensor_scan(
                            out=bl[:, n0 + 128 * q:n0 + 128 * q + 128], data0=ones[:, :],
                            data1=stmp[:, 1, 128 * q:128 * q + 128], initial=0.0, op0=ALU.mult, op1=ALU.add),
                            reads=["stmp1", "ones"], writes=["bl"])
                else:
                    S.op("dve", lambda e: e.tensor_tensor_scan(
                        out=bl[:, 2048:2064], data0=ones[:, 0:16], data1=stmp[:, 1, 0:16], initial=0.0,
                        op0=ALU.mult, op1=ALU.add), reads=["stmp1", "ones"], writes=["bl"])
                    S.op("dve", lambda e: e.tensor_copy(out=bl[:, 2064:2096], in_=stmp[:, 1, 16:48]),
                         reads=["stmp1"], writes=["bl"])
            blc = bl[:, 0:2048].rearrange("p (i c) -> p i c", c=128)[:, :, 127]
            S.op("act", lambda e, blc=blc: e.activation(out=elast[:, 0:16], in_=blc, func=AF.Exp, scale=-1.0 / 16),
                 reads=["bl"], writes=["elast"])
            S.op("act", lambda e: e.activation(out=elast[:, 16:17], in_=bl[:, TP - 1:TP], func=AF.Exp,
                                               scale=-1.0 / 16), reads=["bl"], writes=["elast"])
            for n in range(5):
                n0, N = NTL[n]
                bq = nextbank()
                proj_fm(WB, "ws", 128 * hp, 128, hT, "hT", n, bq)
                bk = nextbank()
                proj_fm(WB, "ws", 256 + 128 * hp, 128, hT, "hT", n, bk)
                S.op("act", lambda e, n0=n0, N=N: e.activation(out=stmp[:, 0, 0:N], in_=bl[:, n0:n0 + N],
                                                               func=AF.Exp, scale=-1.0 / 16),
                     reads=["bl"], writes=["stmp0"])
                S.op("act", lambda e, n0=n0, N=N: e.activation(out=stmp[:, 1, 0:N], in_=bl[:, n0:n0 + N],
                                                               func=AF.Exp, scale=1.0 / 16),
                     reads=["bl"], writes=["stmp1"])
                S.op("dve", lambda e, bq=bq, n0=n0, N=N: e.scalar_tensor_tensor(
                    out=qT_[:, n0:n0 + N], in0=ps[:, bq, 0:N], scalar=0.125, in1=stmp[:, 0, 0:N],
                    op0=ALU.mult, op1=ALU.mult), reads=["bank%d" % bq, "stmp0"], writes=["qT"])
                S.op("dve", lambda e, bk=bk, n0=n0, N=N: e.tensor_tensor(
                    out=kT_[:, n0:n0 + N], in0=ps[:, bk, 0:N], in1=stmp[:, 1, 0:N], op=ALU.mult),
                    reads=["bank%d" % bk, "stmp1"], writes=["kT"])
                if n == 4:
                    S.op("dve", lambda e, bq=bq: e.tensor_scalar(out=qs[:, :], in0=ps[:, bq, 32:48], scalar1=0.125,
                                                                 scalar2=None, op0=ALU.mult),
                         reads=["bank%d" % bq], writes=["qs"])
                    S.op("dve", lambda e: e.tensor_copy(out=ebs[:, :], in_=stmp[:, 0, 32:48]),
                         reads=["stmp0"], writes=["ebs"])
            for g4 in range(5):
                b = nextbank()
                pb = ps[:, b, :].bitcast(BF16)
                ids = list(range(4 * g4, min(4 * g4 + 4, 17)))
                for q, i in enumerate(ids):
                    c0, P = TILES[i]
                    P = 128 if i < 16 else 16
                    S.op("pe", lambda e, q=q, c0=c0, P=P, pb=pb: e.transpose(
                        out=pb[0:P, q * 128:q * 128 + 128], in_=kT_[:, c0:c0 + P], identity=identb[:, :]),
                        reads=["kT", "identb"], writes=["bank%d" % b], signal=(i == ids[-1]))
                for q, i in enumerate(ids):
                    P = 128 if i < 16 else 16
                    S.op("act", lambda e, q=q, i=i, P=P, pb=pb: e.activation(
                        out=ktok[0:P, i, :], in_=pb[0:P, q * 128:q * 128 + 128], func=AF.Copy),
                        reads=["bank%d" % b], writes=["ktok"])
            vproj(hp, [16] if hp == 0 else range(17), [])
            if hp == 0:
                for (cc0, cn, dc0) in ((256, 256, 0), (512, 512, 256)):
                    b = nextbank()
                    for k in range(8):
                        S.op("pe", lambda e, k=k, b=b, cc0=cc0, cn=cn: e.matmul(
                            ps[0:48, b, 0:cn], lhsT=hT[:, k, 2048:2096], rhs=WB[:, k, cc0:cc0 + cn],
                            start=(k == 0), stop=(k == 7)),
                            reads=["ws", "hT%d" % k], writes=["bank%d" % b], signal=(k == 7))
                    S.op("act", lambda e, b=b, cn=cn, dc0=dc0: e.activation(out=ksv[:, dc0:dc0 + cn], in_=ps[0:48, b, 0:cn],
                                                                            func=AF.Copy),
                         reads=["bank%d" % b], writes=["ksv"])
                    if dc0 == 0:
                        S.op("act", lambda e, b=b: e.activation(out=ksb[:, :], in_=ps[0:48, b, 0:256], func=AF.Copy),
                             reads=["bank%d" % b], writes=["ksb"])
            if hp == 1:
                S.dma("pool", ws[:], w_out.rearrange("(ko p) c -> p ko c", p=128), writes=["ws"])
            SB = [0, 1, 2, 3]
            for hl in range(2):
                h = 2 * hp + hl
                S.op("dve", lambda e, h=h: e.tensor_tensor(
                    out=Vexp, in0=ksv[32:48, 256 + 128 * h:384 + 128 * h].unsqueeze(1).to_broadcast([16, 16, 128]),
                    in1=ident[32:48, 32:48].unsqueeze(2).to_broadcast([16, 16, 128]), op=ALU.mult),
                    reads=["ksv", "ident"], writes=["Vexp"])
                for n4 in range(4):
                    S.op("pe", lambda e, n4=n4, h=h, hl=hl: e.matmul(
                        ps[64 * hl:64 * hl + 64, SB[n4], :], lhsT=ksb[32:48, 64 * h:64 * h + 64],
                        rhs=Vexp[:, 4 * n4:4 * n4 + 4, :].rearrange("p j v -> p (j v)"), start=True, stop=True),
                        reads=["ksb", "Vexp"], writes=["bank%d" % SB[n4]])
            S.op("dve", lambda e: e.memset(S32[:], 0.0), reads=["S32"], writes=["S32"])
            SCB = [4, 5]
            RB = 6
            PBK = 7

            def emit_P(g4):
                ids = list(range(4 * g4, min(4 * g4 + 4, 17)))
                for q, i in enumerate(ids):
                    C = 128 if i < 16 else 16
                    for hl in range(2):
                        S.op("pe", lambda e, hl=hl, i=i, C=C, q=q: e.matmul(
                            ps[64 * hl:64 * hl + 64, PBK, 128 * q:128 * q + 128],
                            lhsT=ktok[0:C, i, 64 * hl:64 * hl + 64],
                            rhs=vt(i, C)[:, 128 * hl:128 * hl + 128], start=True, stop=True),
                            reads=["ktok", "vtok"], writes=["bank%d" % PBK],
                            signal=(i == ids[-1] and hl == 1))

            def emit_Schain(i):
                q = i % 4
                S.op("dve", lambda e: e.tensor_tensor(out=Stmp[:, :], in0=S32[:, :],
                                                      in1=ps[:, PBK, 128 * q:128 * q + 128], op=ALU.add),
                     reads=["S32", "bank%d" % PBK], writes=["Stmp", "bank%d" % PBK])
                S.op("dve", lambda e: e.tensor_scalar(out=S32[:, :], in0=Stmp[:, :],
                                                      scalar1=elast[:, i:i + 1], scalar2=None, op0=ALU.mult),
                     reads=["Stmp", "elast"], writes=["S32"])
                if i < 16:
                    S.op("act", lambda e: e.activation(out=SbA[:, i + 1, :], in_=Stmp[:, :], func=AF.Identity,
                                                       scale=elast[:, i:i + 1]),
                         reads=["Stmp", "elast"], writes=["SbA%d" % (i + 1)])

            def OBof(i):
                return [0, 1] if (i // 4) % 2 == 0 else [2, 3]

            def sc_op(qq):
                i, hl = qq // 2, qq % 2
                c0 = 128 * i
                C = 128 if i < 16 else 16
                bs = SCB[qq % 2]
                sp2 = i % 2
                S.op("pe", lambda e: e.matmul(
                    ps[0:C, bs, 0:C], lhsT=kT_[64 * hl:64 * hl + 64, c0:c0 + C],
                    rhs=qT_[64 * hl:64 * hl + 64, c0:c0 + C], start=True, stop=True),
                    reads=["kT", "qT"], writes=["bank%d" % bs])
                S.op("dve", lambda e: e.tensor_tensor(
                    out=scm[0:C, 2 * sp2 + hl, 0:C], in0=ps[0:C, bs, 0:C], in1=tri[0:C, 0:C], op=ALU.mult),
                    reads=["bank%d" % bs, "tri"], writes=["scm%d_%d" % (sp2, hl), "bank%d" % bs])

            def o_op(qq):
                i, hl = qq // 2, qq % 2
                c0 = 128 * i
                C = 128 if i < 16 else 16
                gcol = (i % 4) * 128
                OB = OBof(i)
                sp2 = i % 2
                S.op("pe", lambda e: e.matmul(
                    ps[:, OB[hl], gcol:gcol + C], lhsT=vt(i, C)[:, 128 * hl:128 * hl + 128],
                    rhs=scm[0:C, 2 * sp2 + hl, 0:C], start=True, stop=(i == 0)),
                    reads=["vtok", "scm%d_%d" % (sp2, hl)], writes=["bank%d" % OB[hl]], signal=(i == 0))
                if i > 0:
                    S.op("pe", lambda e: e.matmul(
                        ps[:, OB[hl], gcol:gcol + C], lhsT=SbA[64 * hl:64 * hl + 64, i, :],
                        rhs=qT_[64 * hl:64 * hl + 64, c0:c0 + C], start=False, stop=True),
                        reads=["SbA%d" % i, "qT"], writes=["bank%d" % OB[hl]])

            sqb = [xin[:, 0, 0:512], xin[:, 1, 0:512]]
            rsb = [xin[:, 0, 512:1024], xin[:, 1, 512:1024]]
            tbf = [ht[:, 0, 0:512], ht[:, 1, 0:512]]

            def rms_a(g):
                n0, N = NTL[g]
                OB = OBof(4 * g)
                for hl in range(2):
                    S.op("act", lambda e, hl=hl: e.activation(out=sqb[hl][:, 0:N], in_=ps[:, OB[hl], 0:N],
                                                              func=AF.Square),
                         reads=["bank%d" % OB[hl]], writes=["sqb%d" % hl])

            def rms_b1(g):
                n0, N = NTL[g]
                for hl in range(2):
                    S.op("pe", lambda e, hl=hl: e.matmul(ps[:, RB, 0:N], lhsT=ones[:, :], rhs=sqb[hl][:, 0:N],
                                                         start=True, stop=True),
                         reads=["ones", "sqb%d" % hl], writes=["bank%d" % RB])
                    S.op("act", lambda e, hl=hl: e.activation(out=rsb[hl][:, 0:N], in_=ps[:, RB, 0:N], func=AF.Ln,
                                                              scale=1.0 / 128, bias=DV(12)),
                         reads=["bank%d" % RB, "dvc"], writes=["rsb%d" % hl])
                    S.op("act", lambda e, hl=hl: e.activation(out=rsb[hl][:, 0:N], in_=rsb[hl][:, 0:N], func=AF.Exp,
                                                              scale=-0.5),
                         reads=["rsb%d" % hl], writes=["rsb%d" % hl])

            def rms_b2(g):
                n0, N = NTL[g]
                OB = OBof(4 * g)
                for hl in range(2):
                    h = 2 * hp + hl
                    ob = "bank%d" % OB[hl]
                    S.op("dve", lambda e, hl=hl, h=h: e.scalar_tensor_tensor(
                        out=tbf[hl][:, 0:N], in0=ps[:, OB[hl], 0:N], scalar=PV_GNG(h), in1=rsb[hl][:, 0:N],
                        op0=ALU.mult, op1=ALU.mult), reads=[ob, "rsb%d" % hl, "pvec"], writes=["tbf%d" % hl, ob])
                    S.op("pool", lambda e, hl=hl, h=h: e.tensor_tensor(
                        out=ymT[:, 4 + h, n0:n0 + N], in0=tbf[hl][:, 0:N], in1=sgv[:, hl, n0:n0 + N],
                        op=ALU.mult), reads=["tbf%d" % hl, "sg"], writes=["ymT%d" % (4 + h)])

            emit_P(0)
            sc_op(0)
            for n4 in range(4):
                S0v = S0[:, 4 * n4:4 * n4 + 4, :]
                S.op("dve", lambda e, n4=n4, S0v=S0v: e.tensor_tensor(
                    out=S0v, in0=S0v, in1=ebs[:, 4 * n4:4 * n4 + 4].unsqueeze(2).to_broadcast([128, 4, 128]),
                    op=ALU.mult), reads=S0k + ["ebs"], writes=S0k)
                S.op("dve", lambda e, n4=n4, S0v=S0v: e.tensor_tensor(
                    out=S0v, in0=S0v, in1=ps[:, SB[n4], :].rearrange("p (j v) -> p j v", v=128), op=ALU.add),
                    reads=S0k + ["bank%d" % SB[n4]], writes=S0k)
            for hl in range(2):
                S.dma("sp", gla_s.rearrange("j h d v -> h d j v")[2 * hp + hl], S0[64 * hl:64 * hl + 64, :, :],
                      reads=S0k)
            S.op("act", lambda e, S0=S0: e.activation(out=S0b, in_=S0, func=AF.Copy), reads=S0k, writes=["S0b"])
            pending = None
            pend2 = None
            for qq in range(34):
                i, hl = qq // 2, qq % 2
                if hl == 0:
                    emit_Schain(i)
                    if i % 4 == 3 and i < 16:
                        emit_P(i // 4 + 1)
                    if i == 16:
                        for h2 in range(2):
                            S.op("dve", lambda e, h2=h2: e.memset(ps[:, OBof(16)[h2], 0:48], 0.0),
                                 reads=["bank%d" % OBof(16)[h2]], writes=["bank%d" % OBof(16)[h2]])
                if qq + 1 < 34:
                    sc_op(qq + 1)
                o_op(qq)
                if hl == 1:
                    if pend2 is not None:
                        rms_b2(pend2)
                        pend2 = None
                    if pending is not None:
                        rms_b1(pending)
                        pend2 = pending
                        pending = None
                    if i % 4 == 3:
                        rms_a(i // 4)
                        pending = i // 4
            if pend2 is not None:
                rms_b2(pend2)
            if pending is not None:
                rms_b1(pending)
                rms_b2(pending)
            for hl in range(2):
                for j in range(NS):
                    S.op("pe", lambda e, hl=hl, j=j, S0=S0: e.matmul(
                        ps[:, OBof(16)[hl], 32 + j:33 + j], lhsT=S0b[64 * hl:64 * hl + 64, j, :],
                        rhs=qs[64 * hl:64 * hl + 64, j:j + 1], start=True, stop=True),
                        reads=["S0b", "qs"], writes=["bank%d" % OBof(16)[hl]], signal=(j == NS - 1))
            rms_a(4)
            rms_b1(4)
            rms_b2(4)
            S.dma("sp", gla_p.rearrange("h d v -> (h d) v")[128 * hp:128 * hp + 128, :], S32[:, :], reads=["S32"])

        load_gb(0, lng[0], ALPHA)
        load_gb(1, lng[1], ALPHA)
        S.barrier()
        wgu_v = wgu[:, 0:4096].rearrange("p (s g k c) -> p s g k c", s=2, g=2, k=8)
        wg_r = wg.rearrange("(ko p) f -> p ko f", p=128)
        wu_r = wu.rearrange("(ko p) f -> p ko f", p=128)
        flist = [f0 + fl for (f0, nf) in FPASS for fl in range(nf)]
        wgu_emitted = [0]

        def emit_wgu_upto(idx):
            while wgu_emitted[0] <= min(idx, NF - 1):
                j = wgu_emitted[0]
                f = flist[j]
                S.dma("pool", wgu_v[:, j % 2, 0, :, :], wg_r[:, :, 128 * f:128 * f + 128], writes=["wgu%d" % (j % 2)])
                S.dma("pool", wgu_v[:, j % 2, 1, :, :], wu_r[:, :, 128 * f:128 * f + 128], writes=["wgu%d" % (j % 2)])
                wgu_emitted[0] += 1

        emit_wgu_upto(1)
        ymk = ["ymT%d" % k for k in range(8)]
        pair_ctr = [0]

        def nextpair():
            b = (pair_ctr[0] % 4) * 2
            pair_ctr[0] += 1
            return b

        def D0_front(i):
            c0, P = TILES[i]
            q = i % 4
            xb, xk = xbuf[q], "xb%d" % q
            load_x_buf(i)
            mb0 = 2 * (i % 2)
            mb = [mb0, mb0 + 1]
            for half in range(2):
                S.op("pe", lambda e, half=half: e.matmul(
                    ps[0:P, mb[half], :], lhsT=onesb[0:33, 0:P], rhs=hlb[0:33, 0, 512 * half:512 * half + 512],
                    start=True, stop=False), reads=["onesb", "brow0"], writes=["bank%d" % mb[half]], signal=False)
                for k in range(8):
                    S.op("pe", lambda e, k=k, half=half: e.matmul(
                        ps[0:P, mb[half], :], lhsT=ymT[:, k, c0:c0 + P], rhs=ws[:, k, 512 * half:512 * half + 512],
                        start=False, stop=(k == 7)),
                        reads=["ws", ymk[k]], writes=["bank%d" % mb[half]], signal=(k == 7))
            S.op("act", lambda e: e.activation(
                out=xb[:P, :], in_=xb[:P, :], func=AF.Identity,
                scale=stat0[:P, i, 0:1], bias=stat0[:P, i, 1:2]),
                reads=[xk, "st0_%dr" % i, "st0_%dn" % i], writes=[xk])

        def D0_mult(i):
            c0, P = TILES[i]
            q = i % 4
            xb, xk = xbuf[q], "xb%d" % q
            S.op("dve", lambda e: e.tensor_tensor(out=xb[:P, :], in0=xb[:P, :], in1=gb[:P, 0, :], op=ALU.mult),
                 reads=[xk, "gb0"], writes=[xk])

        def D1a(i):
            c0, P = TILES[i]
            q, par = i % 4, i % 2
            xb, xk = xbuf[q], "xb%d" % q
            mb0 = 2 * par
            S.op("dve", lambda e: e.tensor_tensor(
                out=xb[:P, :].rearrange("p (a c) -> p a c", c=512),
                in0=xb[:P, :].rearrange("p (a c) -> p a c", c=512),
                in1=ps[0:P, mb0:mb0 + 2, :], op=ALU.add),
                reads=[xk, "bank%d" % mb0, "bank%d" % (mb0 + 1)], writes=[xk])
            kb = "bst%d" % par
            S.op("dve", lambda e: e.bn_stats(out=bst[:P, par, 0, :], in_=xb[:P, 0:512]), reads=[xk], writes=[kb + "a"])
            S.op("dve", lambda e: e.bn_stats(out=bst[:P, par, 1, :], in_=xb[:P, 512:1024]), reads=[xk],
                 writes=[kb + "b"])
            S.op("dve", lambda e: e.bn_aggr(out=mv[:P, par, :], in_=bst[:P, par, :, :].rearrange("p a b -> p (a b)")),
                 reads=[kb + "a", kb + "b"], writes=["mv%d" % par])

        def D1b_act(i):
            c0, P = TILES[i]
            par = i % 2
            S.op("act", lambda e: e.activation(out=sd[:P, par, 0:1], in_=mv[:P, par, 1:2], func=AF.Sqrt,
                                               bias=dvec[:P, 10:11]),
                 reads=["mv%d" % par, "dvc"], writes=["sd%d" % par])

        def D1b_dve(i):
            c0, P = TILES[i]
            par = i % 2
            key = "st1_%d" % par
            S.op("dve", lambda e: e.reciprocal(out=rn[:P, par, 0:1], in_=sd[:P, par, 0:1]),
                 reads=["sd%d" % par], writes=[key + "r"])
            S.op("dve", lambda e: e.tensor_scalar(out=rn[:P, par, 1:2], in0=mv[:P, par, 0:1],
                                                  scalar1=rn[:P, par, 0:1], scalar2=-1.0, op0=ALU.mult, op1=ALU.mult),
                 reads=["mv%d" % par, key + "r"], writes=[key + "n"])

        def D1c(i):
            c0, P = TILES[i]
            q, par = i % 4, i % 2
            xb, xk = xbuf[q], "xb%d" % q
            hi = big[:P, i * 1024:(i + 1) * 1024]
            S.op("act", lambda e: e.activation(
                out=hi, in_=xb[:P, :], func=AF.Identity, scale=rn[:P, par, 0:1], bias=rn[:P, par, 1:2]),
                reads=[xk, "st1_%dr" % par, "st1_%dn" % par], writes=["big%d" % i])

        def D2(i):
            c0, P = TILES[i]
            hi = big[:P, i * 1024:(i + 1) * 1024]
            transpose_tile(hi, P, c0, hT, ["big%d" % i], "hT", 56, tbank)

        for s_ in range(17 + 4):
            for fn, sk in ((D2, 4), (D1c, 3), (D1b_act, 2), (D0_front, 0), (D1a, 1), (D1b_dve, 2), (D0_mult, 0)):
                if 0 <= s_ - sk < 17:
                    fn(s_ - sk)

        wd_r = wd.rearrange("(f p) d -> p f d", p=128)
        load_gb(0, lng[2])
        load_brow(1, lnb[1], ALPHA)
        emit_wgu_upto(1)
        S.dma("pool", ws[:, 0:FPASS[0][1], :], wd_r[:, 0:FPASS[0][1], :], writes=["ws"])
        fcount = 0
        for pi, (f0, nf) in enumerate(FPASS):
            if pi == 1:
                load_gb(1, lnb[2])
            for fl in range(nf):
                f = f0 + fl
                slot = fcount % 2
                emit_wgu_upto(fcount + 1)
                fcount += 1
                wk = "wgu%d" % slot
                for n in range(5):
                    n0, N = NTL[n]
                    bg, bu = nextbank(), nextbank()
                    for (g, b) in ((0, bg), (1, bu)):
                        for k in range(8):
                            S.op("pe", lambda e, k=k, g=g, b=b, n0=n0, N=N, slot=slot: e.matmul(
                                ps[:, b, 0:N], lhsT=wgu_v[:, slot, g, k, :], rhs=hT[:, k, n0:n0 + N],
                                start=(k == 0), stop=(k == 7)),
                                reads=[wk, "hT%d" % k], writes=["bank%d" % b], signal=(k == 7))
                    sp_ = n % 2
                    S.op("act", lambda e, bg=bg, N=N, sp_=sp_: e.activation(out=stmp[:, sp_, 0:N], in_=ps[:, bg, 0:N],
                                                                           func=AF.Silu),
                         reads=["bank%d" % bg], writes=["stmp%d" % sp_])
                    S.op("dve", lambda e, bu=bu, n0=n0, N=N, fl=fl, sp_=sp_: e.tensor_tensor(
                        out=ymT[:, fl, n0:n0 + N], in0=stmp[:, sp_, 0:N], in1=ps[:, bu, 0:N], op=ALU.mult),
                        reads=["stmp%d" % sp_, "bank%d" % bu], writes=["ymT%d" % fl])
            last = (f0 + nf == NF)
            first = (f0 == 0)
            emit_wgu_upto(fcount + 1)

            def T0(i, nf=nf, first=first, last=last):
                c0, P = TILES[i]
                par = i % 2
                hi = big[:P, i * 1024:(i + 1) * 1024]
                bk = "big%d" % i
                mb0 = nextpair()
                for half in range(2):
                    b = mb0 + half
                    if first:
                        S.op("pe", lambda e, half=half, b=b: e.matmul(
                            ps[0:P, b, :], lhsT=onesb[0:33, 0:P], rhs=hlb[0:33, 1, 512 * half:512 * half + 512],
                            start=True, stop=False), reads=["onesb", "brow1"], writes=["bank%d" % b], signal=False)
                    for fl in range(nf):
                        S.op("pe", lambda e, fl=fl, half=half, b=b: e.matmul(
                            ps[0:P, b, :], lhsT=ymT[:, fl, c0:c0 + P], rhs=ws[:, fl, 512 * half:512 * half + 512],
                            start=(fl == 0 and not first), stop=(fl == nf - 1)),
                            reads=["ws", "ymT%d" % fl], writes=["bank%d" % b], signal=(fl == nf - 1))
                if first:
                    S.op("dve", lambda e: e.tensor_tensor(out=hi, in0=hi, in1=gb[:P, 1, :], op=ALU.mult),
                         reads=[bk, "gb1"], writes=[bk])
                if not last:
                    S.op("dve", lambda e: e.tensor_tensor(
                        out=hi.rearrange("p (a c) -> p a c", c=512), in0=hi.rearrange("p (a c) -> p a c", c=512),
                        in1=ps[0:P, mb0:mb0 + 2, :], op=ALU.add),
                        reads=[bk, "bank%d" % mb0, "bank%d" % (mb0 + 1)], writes=[bk])
                else:
                    S.op("dve", lambda e: e.scalar_tensor_tensor(
                        out=hi.rearrange("p (a c) -> p a c", c=512), in0=ps[0:P, mb0:mb0 + 2, :], scalar=1.0,
                        in1=hi.rearrange("p (a c) -> p a c", c=512), op0=ALU.mult, op1=ALU.add,
                        accum_out=ssum[:P, par, 0:1]),
                        reads=[bk, "bank%d" % mb0, "bank%d" % (mb0 + 1)], writes=[bk, "ssum%d_0" % par])
                    S.op("act", lambda e: e.activation(out=xin[:P, par, :], in_=hi, func=AF.Square,
                                                       accum_out=ssum[:P, par, 1:2]),
                         reads=[bk], writes=["xinj%d" % par, "ssum%d_1" % par])

            def T1(i):
                c0, P = TILES[i]
                par = i % 2
                key = "st2_%d" % par
                S.op("dve", lambda e: e.tensor_scalar(out=mv[:P, par, 0:1], in0=ssum[:P, par, 0:1], scalar1=1.0 / D,
                                                      scalar2=None, op0=ALU.mult),
                     reads=["ssum%d_0" % par], writes=["mv%d" % par])
                S.op("dve", lambda e: e.tensor_tensor(out=ssum[:P, par, 2:3], in0=mv[:P, par, 0:1],
                                                      in1=mv[:P, par, 0:1], op=ALU.mult),
                     reads=["mv%d" % par], writes=["ssum%d_2" % par])
                S.op("dve", lambda e: e.scalar_tensor_tensor(out=mv[:P, par, 1:2], in0=ssum[:P, par, 1:2],
                                                             scalar=1.0 / D, in1=ssum[:P, par, 2:3],
                                                             op0=ALU.mult, op1=ALU.subtract),
                     reads=["ssum%d_1" % par, "ssum%d_2" % par, "mv%d" % par], writes=["mv%d" % par])
                S.op("act", lambda e: e.activation(out=sd[:P, par, 0:1], in_=mv[:P, par, 1:2], func=AF.Sqrt,
                                                   bias=dvec[:P, 10:11]),
                     reads=["mv%d" % par, "dvc"], writes=["sd%d" % par])
                S.op("dve", lambda e: e.reciprocal(out=rn[:P, par, 0:1], in_=sd[:P, par, 0:1]),
                     reads=["sd%d" % par], writes=[key + "r"])
                S.op("dve", lambda e: e.tensor_scalar(out=rn[:P, par, 1:2], in0=mv[:P, par, 0:1],
                                                      scalar1=rn[:P, par, 0:1], scalar2=-1.0,
                                                      op0=ALU.mult, op1=ALU.mult),
                     reads=["mv%d" % par, key + "r"], writes=[key + "n"])

            def T2(i):
                c0, P = TILES[i]
                par = i % 2
                hi = big[:P, i * 1024:(i + 1) * 1024]
                S.op("act", lambda e: e.activation(
                    out=ht[:P, par, :], in_=hi, func=AF.Identity, scale=rn[:P, par, 0:1], bias=rn[:P, par, 1:2]),
                    reads=["big%d" % i, "st2_%dr" % par, "st2_%dn" % par], writes=["ht%d" % par])

            def T3(i):
                c0, P = TILES[i]
                par = i % 2
                hk = "ht%d" % par
                S.op("dve", lambda e: e.tensor_tensor(out=ht[:P, par, :], in0=ht[:P, par, :],
                                                      in1=gb[:P, 0, :], op=ALU.mult),
                     reads=[hk, "gb0"], writes=[hk])
                S.op("dve", lambda e: e.tensor_tensor(out=ht[:P, par, :], in0=ht[:P, par, :],
                                                      in1=gb[:P, 1, :], op=ALU.add),
                     reads=[hk, "gb1"], writes=[hk])
                if i == 0:
                    S.dma("sp", y_p[0:112, :], ht[16:128, par, :], reads=[hk])
                elif i < 16:
                    S.dma("sp", y_p[128 * i - 16:128 * i + 112, :], ht[:, par, :], reads=[hk])
                else:
                    S.dma("sp", y_p[2032:2048, :], ht[0:16, par, :], reads=[hk])
                    S.dma("sp", y_s, ht[32:48, par, :], reads=[hk])

            if last:
                pipeline([T0, T1, T2, T3], 17)
            else:
                for i in range(17):
                    T0(i)
            if pi + 1 < len(FPASS):
                nf0, nnf = FPASS[pi + 1]
                S.dma("pool", ws[:, 0:nnf, :], wd_r[:, nf0:nf0 + nnf, :], writes=["ws"])

        S.finish()
        with nc.Block() as block:
            S.emit(block, sems)
    return nc


_NC_CACHE = {}


def kernel(x_prompt, x_sample, state_gla, state_lru, state_conv, meta_tokens, ln_in_g, ln_in_b,
           w_in, conv_w, conv_b, lru_gate_a_w, lru_gate_a_b, lru_gate_x_w, lru_gate_x_b, lru_lambda,
           gla_alpha_w, gla_alpha_b, gla_norm_g, w_out, ln1_g, ln1_b, w_ffn_gate, w_ffn_up, w_ffn_down,
           ln2_g, ln2_b):
    f = lambda a: np.ascontiguousarray(np.asarray(a, dtype=np.float32))
    n = 8
    pvec = np.zeros((128, NPV), np.float32)
    cw = f(conv_w)[0]
    for c in range(4):
        sl = slice(128 * c, 128 * c + 128)
        for j in range(4):
            pvec[:, c * 8 + j] = cw[j, sl]
        pvec[:, c * 8 + 4] = f(conv_b)[0, sl]
        pvec[:, c * 8 + 5] = f(lru_gate_a_b)[0, sl]
        pvec[:, c * 8 + 6] = f(lru_gate_x_b)[0, sl]
        pvec[:, c * 8 + 7] = f(lru_lambda)[0, sl]
    for hp in range(2):
        pvec[:, 32 + hp] = f(gla_alpha_b)[0, 128 * hp:128 * hp + 128]
    for h in range(4):
        pvec[:, 34 + h] = f(gla_norm_g)[0, 128 * h:128 * h + 128]
    for k in range(8):
        sl = slice(128 * k, 128 * k + 128)
        pvec[:, 40 + k] = f(ln_in_g)[sl]
        pvec[:, 48 + k] = f(ln_in_b)[sl]
        pvec[:, 56 + k] = f(ln1_g)[0, sl]
        pvec[:, 64 + k] = f(ln1_b)[0, sl]
    ident = np.eye(128, dtype=np.float32)
    tri = np.triu(np.ones((128, 128), np.float32))
    shared = {
        "meta": f(meta_tokens), "ln_in_g": f(ln_in_g), "ln_in_b": f(ln_in_b),
        "ln1_g": f(ln1_g)[0], "ln1_b": f(ln1_b)[0], "ln2_g": f(ln2_g)[0], "ln2_b": f(ln2_b)[0],
        "w_in": f(w_in)[0], "gaw": f(lru_gate_a_w)[0], "gxw": f(lru_gate_x_w)[0], "alw": f(gla_alpha_w)[0],
        "pvec": pvec, "w_out": f(w_out)[0], "wg": f(w_ffn_gate)[0], "wu": f(w_ffn_up)[0], "wd": f(w_ffn_down)[0],
        "ident": ident, "tri": tri,
    }
    xpr, xsa = f(x_prompt), f(x_sample)
    sgl, slr, scv = f(state_gla)[0], f(state_lru)[0], f(state_conv)[0]
    in_maps = []
    for c in range(n):
        js = slice(NS * c, NS * c + NS)
        m = dict(shared)
        m["xp"] = xpr[c]
        m["xs"] = np.ascontiguousarray(xsa[js, 0, :])
        m["sg"] = np.ascontiguousarray(sgl[js])
        m["slT"] = np.ascontiguousarray(slr[js].reshape(NS, 4, 128).transpose(2, 1, 0))
        m["scT"] = np.ascontiguousarray(scv[js].reshape(NS, 3, 4, 128).transpose(3, 2, 1, 0))
        in_maps.append(m)
    if "nc" not in _NC_CACHE:
        _NC_CACHE["nc"] = build_nc()
    res = run_bass_kernel_spmd(_NC_CACHE["nc"], in_maps, core_ids=list(range(n)))
    R = res.results
    y_prompt = np.stack([R[c]["y_p"] for c in range(n)], 0)
    y_sample = np.concatenate([R[c]["y_s"] for c in range(n)], 0)[:, None, :]
    gla_prompt = np.stack([R[c]["gla_p"] for c in range(n)], 0)[None]
    lru_prompt = np.stack([R[c]["osm"][:, :, 16].T.reshape(512) for c in range(n)], 0)[None]
    conv_prompt = np.stack([R[c]["osm"][:, :, 17:20].transpose(2, 1, 0).reshape(3, 512) for c in range(n)], 0)[None]
    gla_sample = np.concatenate([R[c]["gla_s"] for c in range(n)], 0)[None]
    lru_sample = np.concatenate([R[c]["osm"][:, :, 0:16].transpose(2, 1, 0).reshape(NS, 512) for c in range(n)], 0)[None]
    conv_sample = np.concatenate([R[c]["conv_s"].transpose(3, 2, 1, 0).reshape(NS, 3, 512) for c in range(n)], 0)[None]
    outs = (y_prompt, y_sample, gla_prompt, lru_prompt, conv_prompt, gla_sample, lru_sample, conv_sample)
    return tuple(np.ascontiguousarray(o, dtype=np.float32) for o in outs)
```

```python
import contextlib
import numpy as np
import concourse.bass as bass
import concourse.mybir as mybir
from concourse.bass_utils import run_bass_kernel_spmd

F32 = mybir.dt.float32
BF16 = mybir.dt.bfloat16
AF = mybir.ActivationFunctionType
ALU = mybir.AluOpType

D = 1024
TX = 2048
NMETA = 16
TP = NMETA + TX
NT = 2096
SC0 = 2080
NS = 16
DFF = 2816
NF = DFF // 128
INC = 2576
ALPHA = float(2.0 ** 0.25)
LN_EPS = 1e-5
RMS_EPS = 1e-6
TILES = [(i * 128, 128) for i in range(16)] + [(2048, 48)]
NTL = [(0, 512), (512, 512), (1024, 512), (1536, 512), (2048, 48)]
FPASS = [(0, 8), (8, 7), (15, 7)]
NPV = 72


class Sched:
    ENGS = ["pe", "act", "dve", "pool", "sp"]

    def __init__(self, K=8):
        self.K = K
        self.prog = {e: [] for e in self.ENGS}
        self.tick = {e: 0 for e in self.ENGS}
        self.waited = {e: {} for e in self.ENGS}
        self.last_w = {}
        self.readers = {}
        self.pending = {e: [] for e in self.ENGS}
        self.dma_cnt = {e: 0 for e in self.ENGS}
        self.final = {}

    def sem_names(self):
        names = ["c_" + e for e in self.ENGS]
        for e in ("sp", "pool", "act"):
            names += [f"d_{e}{s}" for s in range(self.K)]
        return names

    def _deps(self, reads, writes):
        deps = []
        for r in reads:
            t = self.last_w.get(r)
            if t is not None:
                deps.append(("raw", t))
        for w in writes:
            t = self.last_w.get(w)
            if t is not None:
                deps.append(("waw", t))
            for t in self.readers.get(w, {}).values():
                deps.append(("war", t))
        return deps

    def _emit_waits(self, eng, deps, for_dma=False):
        need = {}
        for kind, t in deps:
            sname, val, teng, is_dma = t
            if teng == eng and not is_dma and not for_dma:
                if eng == "pe":
                    continue
            assert val is not None, f"unresolved dep on {sname} from {eng}"
            if need.get(sname, 0) < val:
                need[sname] = val
        for sname, val in need.items():
            if self.waited[eng].get(sname, 0) < val:
                self.waited[eng][sname] = val
                self.prog[eng].append(("wait", sname, val))

    def _update(self, tok, reads, writes):
        for r in reads:
            self.readers.setdefault(r, {})[tok[0]] = tok
        for w in writes:
            self.last_w[w] = tok
            self.readers[w] = {}

    def op(self, eng, fn, reads=(), writes=(), signal=True):
        self._emit_waits(eng, self._deps(reads, writes))
        if signal:
            self.tick[eng] += 1
            tok = ["c_" + eng, self.tick[eng], eng, False]
            for p in self.pending[eng]:
                p[1] = self.tick[eng]
            self.pending[eng] = []
        else:
            tok = ["c_" + eng, None, eng, False]
            self.pending[eng].append(tok)
        self.prog[eng].append(("op", fn, signal))
        self._update(tok, reads, writes)

    def dma(self, eng, out, in_, reads=(), writes=()):
        j = self.dma_cnt[eng]
        self.dma_cnt[eng] += 1
        s, m = j % self.K, j // self.K
        sname = f"d_{eng}{s}"
        deps = self._deps(reads, writes)
        if m > 0:
            deps.append(("raw", [sname, 16 * m, eng, True]))
        self._emit_waits(eng, deps, for_dma=True)
        tok = [sname, 16 * (m + 1), eng, True]
        self.prog[eng].append(("dma", out, in_, sname))
        self._update(tok, reads, writes)
        self.final[sname] = 16 * (m + 1)

    def barrier(self):
        for e in self.ENGS:
            assert not self.pending[e]
        for e in self.ENGS:
            for o in self.ENGS:
                if o != e and self.tick[o] > self.waited[e].get("c_" + o, 0):
                    self.waited[e]["c_" + o] = self.tick[o]
                    self.prog[e].append(("wait", "c_" + o, self.tick[o]))
            for sname, val in self.final.items():
                if self.waited[e].get(sname, 0) < val:
                    self.waited[e][sname] = val
                    self.prog[e].append(("wait", sname, val))

    def finish(self):
        for sname, val in self.final.items():
            if self.waited["sp"].get(sname, 0) < val:
                self.waited["sp"][sname] = val
                self.prog["sp"].append(("wait", sname, val))

    def emit(self, block, sems):
        def run(name, e):
            for it in self.prog[name]:
                if it[0] == "wait":
                    e.wait_ge(sems[it[1]], it[2])
                elif it[0] == "op":
                    inst = it[1](e)
                    if it[2]:
                        inst.then_inc(sems["c_" + name], 1)
                else:
                    e.dma_start(out=it[1], in_=it[2]).then_inc(sems[it[3]], 16)

        @block.sync
        def _(e):
            run("sp", e)

        @block.scalar
        def _(e):
            run("act", e)

        @block.vector
        def _(e):
            run("dve", e)

        @block.gpsimd
        def _(e):
            run("pool", e)

        @block.tensor
        def _(e):
            run("pe", e)


def build_nc():
    nc = bass.Bass("TRN2", target_bir_lowering=False)

    def din(name, shape):
        return nc.dram_tensor(name, list(shape), F32, kind="ExternalInput").ap()

    def dout(name, shape):
        return nc.dram_tensor(name, list(shape), F32, kind="ExternalOutput").ap()

    xp = din("xp", [TX, D])
    xs = din("xs", [NS, D])
    sg_in = din("sg", [NS, 4, 64, 128])
    slT_in = din("slT", [128, 4, NS])
    scT_in = din("scT", [128, 4, 3, NS])
    meta = din("meta", [NMETA, D])
    lng = [din("ln_in_g", [D]), din("ln1_g", [D]), din("ln2_g", [D])]
    lnb = [din("ln_in_b", [D]), din("ln1_b", [D]), din("ln2_b", [D])]
    w_in = din("w_in", [D, INC])
    gaw = din("gaw", [8, 64, 64])
    gxw = din("gxw", [8, 64, 64])
    alw_in = din("alw", [16, 256])
    pvec_in = din("pvec", [128, NPV])
    w_out = din("w_out", [D, D])
    wg = din("wg", [D, DFF])
    wu = din("wu", [D, DFF])
    wd = din("wd", [DFF, D])
    ident_in = din("ident", [128, 128])
    tri_in = din("tri", [128, 128])

    y_p = dout("y_p", [TX, D])
    y_s = dout("y_s", [NS, D])
    gla_p = dout("gla_p", [4, 64, 128])
    osm_out = dout("osm", [128, 4, 20])
    gla_s = dout("gla_s", [NS, 4, 64, 128])
    conv_s = dout("conv_s", [128, 4, 3, NS])

    S = Sched()
    es = contextlib.ExitStack()
    with es:
        def sb(name, shape, dt):
            return es.enter_context(nc.sbuf_tensor("s_" + name, list(shape), dt))

        big = sb("big", [128, 17 * 1024], F32)
        hT = sb("hT", [128, 8, NT], BF16)
        ymT = sb("ymT", [128, 8, NT], BF16)
        ws = sb("ws", [128, 8, 1024], BF16)
        xin = sb("xin", [128, 2, 1024], F32)
        ht = sb("ht", [128, 2, 1024], F32)
        gb = sb("gb", [128, 2, 1024], F32)
        wgu = sb("wgu", [128, 4224], BF16)
        stmp = sb("stmp", [128, 2, 512], F32)
        ident = sb("ident", [128, 128], F32)
        identb = sb("identb", [128, 128], BF16)
        tri = sb("tri", [128, 128], F32)
        ones = sb("ones", [128, 128], F32)
        pvec = sb("pvec", [128, NPV], F32)
        dvec = sb("dvec", [128, 16], F32)
        gw = sb("gw", [128, 8, 128], BF16)
        alw = sb("alw", [16, 256], F32)
        stat0 = sb("stat0", [128, 17, 2], F32)
        bst = sb("bst", [128, 2, 2, 6], F32)
        mv = sb("mv", [128, 2, 2], F32)
        sd = sb("sd", [128, 2, 2], F32)
        rn = sb("rn", [128, 2, 2], F32)
        ssum = sb("ssum", [128, 2, 4], F32)
        S32 = sb("S32", [128, 128], F32)
        Stmp = sb("Stmp", [128, 128], F32)
        Sbf = sb("Sbf", [128, 128], BF16)
        scm = sb("scm", [128, 4, 128], BF16)
        elast = sb("elast", [128, 17], F32)
        qs = sb("qs", [128, NS], BF16)
        ksb = sb("ksb", [48, 256], BF16)
        ebs = sb("ebs", [128, NS], F32)
        smallt = sb("smallt", [128, 4, NS], F32)
        osm = sb("osm", [128, 4, 20], F32)
        brow = sb("brow", [33, 2, 1024], F32)
        hlb = sb("hlb", [33, 2, 1024], BF16)
        onesb = sb("onesb", [33, 128], BF16)
        ps = es.enter_context(nc.psum_tensor("ps", [128, 8, 512], F32))
        sems = {n: es.enter_context(nc.semaphore(n)) for n in S.sem_names()}

        def bank(b):
            return ps[:, b, :]

        bank_ctr = [0]

        def nextbank():
            b = bank_ctr[0] % 8
            bank_ctr[0] += 1
            return b

        def pv(col):
            return pvec[:, col:col + 1]
        PV_CW = lambda c, j: pv(c * 8 + j)
        PV_CB = lambda c: pv(c * 8 + 4)
        PV_GAB = lambda c: pv(c * 8 + 5)
        PV_GXB = lambda c: pv(c * 8 + 6)
        PV_LAM = lambda c: pv(c * 8 + 7)
        PV_ALB = lambda hp: pv(32 + hp)
        PV_GNG = lambda h: pv(34 + h)
        DV = lambda col: dvec[:, col:col + 1]

        S.dma("sp", ident[:], ident_in, writes=["ident"])
        S.dma("sp", tri[:], tri_in, writes=["tri"])
        S.dma("sp", pvec[:], pvec_in, writes=["pvec"])
        S.dma("sp", alw[:], alw_in, writes=["alw"])
        S.dma("pool", identb[:], ident_in, writes=["identb"])
        S.op("dve", lambda e: e.memset(ones[:], 1.0), writes=["ones"])
        S.op("dve", lambda e: e.memset(onesb[:], 1.0), writes=["onesb"])
        S.op("dve", lambda e: e.memset(hlb[:], 0.0), writes=["hlb_init"])
        S.op("dve", lambda e: e.memset(dvec[:, 10:11], LN_EPS), writes=["dvc"])
        S.op("dve", lambda e: e.memset(dvec[:, 11:12], 1.0), writes=["dvc"])
        S.op("dve", lambda e: e.memset(dvec[:, 12:13], RMS_EPS), writes=["dvc"])
        S.op("dve", lambda e: e.memset(S32[:], 0.0), writes=["S32"])
        S.op("pool", lambda e: e.memset(gw[:], 0.0), writes=["gw"])
        for c in range(4):
            for g, src in enumerate((gaw, gxw)):
                for blk in range(2):
                    S.dma("pool", gw[64 * blk:64 * blk + 64, 2 * c + g, 64 * blk:64 * blk + 64],
                          src[2 * c + blk], writes=["gw"])
        for c in range(4):
            S.op("act", lambda e, c=c: e.activation(out=dvec[:, c:c + 1], in_=PV_LAM(c), func=AF.Exp, scale=-1.0),
                 reads=["pvec"], writes=["dv%d" % c])
            S.op("act", lambda e, c=c: e.activation(out=dvec[:, c:c + 1], in_=dvec[:, c:c + 1], func=AF.Ln,
                                                    bias=DV(11)),
                 reads=["dv%d" % c, "dvc"], writes=["dv%d" % c])
            S.op("dve", lambda e, c=c: e.tensor_scalar(out=dvec[:, 4 + c:5 + c], in0=dvec[:, c:c + 1],
                                                       scalar1=-16.0, scalar2=None, op0=ALU.mult),
                 reads=["dv%d" % c], writes=["dw%d" % c])
            S.op("dve", lambda e, c=c: e.tensor_scalar(out=dvec[:, c:c + 1], in0=dvec[:, c:c + 1],
                                                       scalar1=-8.0, scalar2=None, op0=ALU.mult),
                 reads=["dv%d" % c], writes=["dv%d" % c])
        S.op("dve", lambda e: e.tensor_scalar(out=dvec[:, 8:10], in0=pvec[:, 32:34], scalar1=-1.0, scalar2=None,
                                              op0=ALU.mult), reads=["pvec"], writes=["dnalb"])

        def load_gb(slot, src, scale=None):
            S.dma("sp", gb[:, slot, :], src.partition_broadcast(128), writes=["gb%d" % slot])
            if scale is not None:
                S.op("act", lambda e: e.activation(out=gb[:, slot, :], in_=gb[:, slot, :], func=AF.Copy, scale=scale),
                     reads=["gb%d" % slot], writes=["gb%d" % slot])

        def load_brow(slot, src, scale):
            bk_ = "brow%d" % slot
            for p in (0, 32):
                S.dma("sp", brow[p:p + 1, slot, :], src.rearrange("(o d) -> o d", o=1), writes=[bk_ + "_%d" % p])
                S.op("act", lambda e, p=p: e.activation(out=brow[p:p + 1, slot, :], in_=brow[p:p + 1, slot, :],
                                                        func=AF.Copy, scale=scale),
                     reads=[bk_ + "_%d" % p], writes=[bk_ + "_%d" % p])
                S.op("act", lambda e, p=p: e.activation(out=hlb[p:p + 1, slot, :], in_=brow[p:p + 1, slot, :],
                                                        func=AF.Copy),
                     reads=[bk_ + "_%d" % p, "hlb_init"], writes=[bk_])
            S.op("dve", lambda e: e.tensor_tensor(out=brow[32:33, slot, :], in0=brow[32:33, slot, :],
                                                  in1=hlb[32:33, slot, :], op=ALU.subtract),
                 reads=[bk_, bk_ + "_32"], writes=[bk_ + "_32"])
            S.op("act", lambda e: e.activation(out=hlb[32:33, slot, :], in_=brow[32:33, slot, :], func=AF.Copy),
                 reads=[bk_ + "_32"], writes=[bk_])

        def ln_stats(src, P, par, rstd_out, nmr_out, extra_reads, key):
            kb = "bst%d" % par
            S.op("dve", lambda e: e.bn_stats(out=bst[:P, par, 0, :], in_=src[:, 0:512]),
                 reads=extra_reads, writes=[kb + "a"])
            S.op("dve", lambda e: e.bn_stats(out=bst[:P, par, 1, :], in_=src[:, 512:1024]),
                 reads=extra_reads, writes=[kb + "b"])
            S.op("dve", lambda e: e.bn_aggr(out=mv[:P, par, :],
                                            in_=bst[:P, par, :, :].rearrange("p a b -> p (a b)")),
                 reads=[kb + "a", kb + "b"], writes=["mv%d" % par])
            S.op("act", lambda e: e.activation(out=sd[:P, par, 0:1], in_=mv[:P, par, 1:2], func=AF.Sqrt,
                                               bias=dvec[:P, 10:11]),
                 reads=["mv%d" % par, "dvc"], writes=["sd%d" % par])
            S.op("dve", lambda e: e.reciprocal(out=rstd_out, in_=sd[:P, par, 0:1]),
                 reads=["sd%d" % par], writes=[key + "r"])
            S.op("dve", lambda e: e.tensor_scalar(out=nmr_out, in0=mv[:P, par, 0:1], scalar1=rstd_out,
                                                  scalar2=-1.0, op0=ALU.mult, op1=ALU.mult),
                 reads=["mv%d" % par, key + "r"], writes=[key + "n"])

        def load_x_tile(i, par):
            c0, P = TILES[i]
            k = "xin%d" % par
            if i == 0:
                S.dma("sp", xin[0:16, par, :], meta, writes=[k])
                S.dma("sp", xin[16:128, par, :], xp[0:112, :], writes=[k])
            elif i < 16:
                S.dma("sp", xin[:, par, :], xp[128 * i - 16:128 * i + 112, :], writes=[k])
            else:
                S.op("pool", lambda e: e.memset(xin[0:48, par, :], 0.0), writes=[k])
                S.dma("sp", xin[0:16, par, :], xp[2032:2048, :], writes=[k])
                S.dma("sp", xin[32:48, par, :], xs, writes=[k])

        def pipeline(stages, n):
            ns = len(stages)
            for t in range(n + ns - 1):
                for s_ in range(ns - 1, -1, -1):
                    it = t - s_
                    if 0 <= it < n:
                        stages[s_](it)

        tb_ctr = [0]

        def tbank():
            b = 4 + tb_ctr[0] % 4
            tb_ctr[0] += 1
            return b

        def transpose_tile(src, P, c0, dstT, src_keys, dkey, pcol, balloc=None, act_kk=(0, 2)):
            for half in range(2):
                b = (balloc or nextbank)()
                for kk in range(4):
                    k = half * 4 + kk
                    S.op("pe", lambda e, k=k, kk=kk, b=b: e.transpose(
                        out=ps[:, b, kk * 128:kk * 128 + P], in_=src[:, k * 128:(k + 1) * 128],
                        identity=ident[:P, :P]),
                        reads=src_keys + ["ident"], writes=["bank%d" % b], signal=(kk == 3))
                for kk in range(4):
                    k = half * 4 + kk
                    srcv = ps[:, b, kk * 128:kk * 128 + P]
                    dstv = dstT[:, k, c0:c0 + P]
                    gcol, bcol = pv(pcol + k), pv(pcol + 8 + k)
                    if half == 0:
                        S.op("act", lambda e, srcv=srcv, dstv=dstv, gcol=gcol, bcol=bcol: e.activation(
                            out=dstv, in_=srcv, func=AF.Identity, scale=gcol, bias=bcol),
                            reads=["bank%d" % b, "pvec"], writes=["%s%d" % (dkey, k)])
                    else:
                        S.op("dve", lambda e, srcv=srcv, dstv=dstv, gcol=gcol, bcol=bcol: e.tensor_scalar(
                            out=dstv, in0=srcv, scalar1=gcol, scalar2=bcol, op0=ALU.mult, op1=ALU.add),
                            reads=["bank%d" % b, "pvec"], writes=["%s%d" % (dkey, k)])

        def proj_fm(wsrc, wkey, col0, M, xT, xkey, n, b):
            n0, N = NTL[n]
            for k in range(8):
                S.op("pe", lambda e, k=k: e.matmul(ps[0:M, b, 0:N], lhsT=wsrc[:, k, col0:col0 + M],
                                                   rhs=xT[:, k, n0:n0 + N], start=(k == 0), stop=(k == 7)),
                     reads=[wkey, "%s%d" % (xkey, k)], writes=["bank%d" % b], signal=(k == 7))

        S.dma("pool", ws[:], w_in.rearrange("(ko p) c -> p ko c", p=128)[:, :, 0:1024], writes=["ws"])
        S.dma("pool", wgu[:, 0:4224].rearrange("p (k c) -> p k c", c=528),
              w_in.rearrange("(ko p) c -> p ko c", p=128)[:, :, 2048:2576], writes=["wgu"])
        xbuf = [xin[:, 0, :], xin[:, 1, :], ht[:, 0, :], ht[:, 1, :]]

        def load_x_buf(i):
            c0, P = TILES[i]
            xb, xk = xbuf[i % 4], "xb%d" % (i % 4)
            if i == 0:
                S.dma("sp", xb[0:16, :], meta, writes=[xk])
                S.dma("sp", xb[16:128, :], xp[0:112, :], writes=[xk])
            elif i < 16:
                S.dma("sp", xb[:, :], xp[128 * i - 16:128 * i + 112, :], writes=[xk])
            else:
                S.op("pool", lambda e: e.memset(xb[0:48, :], 0.0), writes=[xk])
                S.dma("sp", xb[0:16, :], xp[2032:2048, :], writes=[xk])
                S.dma("sp", xb[32:48, :], xs, writes=[xk])

        def A0a(i):
            c0, P = TILES[i]
            par = i % 2
            xb, xk = xbuf[i % 4], "xb%d" % (i % 4)
            load_x_buf(i)
            kb = "bst%d" % par
            S.op("dve", lambda e: e.bn_stats(out=bst[:P, par, 0, :], in_=xb[:P, 0:512]), reads=[xk], writes=[kb + "a"])
            S.op("dve", lambda e: e.bn_stats(out=bst[:P, par, 1, :], in_=xb[:P, 512:1024]), reads=[xk],
                 writes=[kb + "b"])
            S.op("dve", lambda e: e.bn_aggr(out=mv[:P, par, :], in_=bst[:P, par, :, :].rearrange("p a b -> p (a b)")),
                 reads=[kb + "a", kb + "b"], writes=["mv%d" % par])

        def A0b_act(i):
            c0, P = TILES[i]
            par = i % 2
            S.op("act", lambda e: e.activation(out=sd[:P, par, 0:1], in_=mv[:P, par, 1:2], func=AF.Sqrt,
                                               bias=dvec[:P, 10:11]),
                 reads=["mv%d" % par, "dvc"], writes=["sd%d" % par])

        def A0b_dve(i):
            c0, P = TILES[i]
            par = i % 2
            key = "st0_%d" % i
            S.op("dve", lambda e: e.reciprocal(out=stat0[:P, i, 0:1], in_=sd[:P, par, 0:1]),
                 reads=["sd%d" % par], writes=[key + "r"])
            S.op("dve", lambda e: e.tensor_scalar(out=stat0[:P, i, 1:2], in0=mv[:P, par, 0:1],
                                                  scalar1=stat0[:P, i, 0:1], scalar2=-1.0,
                                                  op0=ALU.mult, op1=ALU.mult),
                 reads=["mv%d" % par, key + "r"], writes=[key + "n"])

        def A1(i):
            c0, P = TILES[i]
            xb, xk = xbuf[i % 4], "xb%d" % (i % 4)
            S.op("act", lambda e: e.activation(
                out=xb[:P, :], in_=xb[:P, :], func=AF.Identity,
                scale=stat0[:P, i, 0:1], bias=stat0[:P, i, 1:2]),
                reads=[xk, "st0_%dr" % i, "st0_%dn" % i], writes=[xk])

        def A2(i):
            c0, P = TILES[i]
            xb, xk = xbuf[i % 4], "xb%d" % (i % 4)
            transpose_tile(xb[:P, :], P, c0, hT, [xk], "hT", 40)

        for s_ in range(17 + 3):
            for fn, sk in ((A1, 2), (A0b_act, 1), (A0a, 0), (A0b_dve, 1), (A2, 3)):
                if 0 <= s_ - sk < 17:
                    fn(s_ - sk)

        RW = 2104
        def R(j, w=NT, off=0):
            return big[:, j * RW + off:j * RW + off + w]
        def Rb(j, half):
            v = big[:, j * RW:j * RW + 2096].bitcast(BF16)
            return v[:, half * NT:(half + 1) * NT]
        S.dma("sp", smallt[:], slT_in, writes=["slT"])
        sct = big[:, 8 * RW:8 * RW + 4 * 3 * NS].rearrange("p (c k j) -> p c k j", c=4, k=3)
        S.dma("sp", sct, scT_in, writes=["scT"])
        S.dma("sp", conv_s[:, :, 0:2, :], sct[:, :, 1:3, :], reads=["scT"])
        RA, RI, RS = 5, 6, 7
        S.op("pool", lambda e: e.memset(R(0, 3), 0.0), writes=["XL"])

        def L0(c):
            p2 = c % 2
            GG = "GG%d" % p2
            for n in range(5):
                n0, N = NTL[n]
                b = nextbank()
                proj_fm(ws, "ws", 128 * c, 128, hT, "hT", n, b)
                S.op("dve", lambda e, b=b, n0=n0, N=N: e.tensor_copy(out=R(0, N, 3 + n0), in_=ps[:, b, 0:N]),
                     reads=["bank%d" % b], writes=["XL"])
            for n in range(5):
                n0, N = NTL[n]
                b = nextbank()
                proj_fm(ws, "ws", 512 + 128 * c, 128, hT, "hT", n, b)
                S.op("act", lambda e, b=b, n0=n0, N=N: e.activation(out=Rb(1 + p2, 0)[:, n0:n0 + N],
                                                                    in_=ps[:, b, 0:N], func=AF.Gelu_apprx_tanh),
                     reads=["bank%d" % b], writes=[GG])

        def L1(c):
            p2 = c % 2
            RU = 3 + p2
            U, UB = "U%d" % p2, "UB%d" % p2
            S.op("dve", lambda e: e.tensor_scalar(out=R(RU), in0=R(0, NT, 0), scalar1=PV_CW(c, 0),
                                                  scalar2=PV_CB(c), op0=ALU.mult, op1=ALU.add),
                 reads=["XL", "pvec"], writes=[U])
            for j in range(1, 4):
                S.op("dve", lambda e, j=j: e.scalar_tensor_tensor(out=R(RU), in0=R(0, NT, j),
                                                                  scalar=PV_CW(c, j), in1=R(RU),
                                                                  op0=ALU.mult, op1=ALU.add),
                     reads=["XL", U, "pvec"], writes=[U])
            us = R(RU, NS, SC0)
            S.op("dve", lambda e: e.tensor_scalar(out=us, in0=R(0, NS, 3 + SC0), scalar1=PV_CW(c, 3),
                                                  scalar2=PV_CB(c), op0=ALU.mult, op1=ALU.add),
                 reads=["XL", U, "pvec"], writes=[U])
            for j in range(3):
                S.op("dve", lambda e, j=j: e.scalar_tensor_tensor(
                    out=us, in0=sct[:, c, j, :], scalar=PV_CW(c, j), in1=us, op0=ALU.mult, op1=ALU.add),
                    reads=["scT", U, "pvec"], writes=[U])
            S.op("dve", lambda e: e.tensor_copy(out=Rb(1 + p2, 1), in_=R(RU)), reads=[U], writes=[UB])
            S.op("act", lambda e: e.activation(out=osm[:, c, 17:20], in_=R(0, 3, 3 + TP - 3), func=AF.Copy),
                 reads=["XL"], writes=["osm"])
            S.dma("sp", conv_s[:, c, 2, :], R(0, NS, 3 + SC0), reads=["XL"])

        def L2a(c):
            p2 = c % 2
            UB = "UB%d" % p2
            for g, dst, bcol, key in ((1, RI, PV_GXB(c), "I"), (0, RA, PV_GAB(c), "A")):
                for n in range(5):
                    n0, N = NTL[n]
                    b = nextbank()
                    S.op("pe", lambda e, b=b, n0=n0, N=N, g=g: e.matmul(
                        ps[:, b, 0:N], lhsT=gw[:, 2 * c + g, :], rhs=Rb(1 + p2, 1)[:, n0:n0 + N], start=True, stop=True),
                        reads=["gw", UB], writes=["bank%d" % b])
                    S.op("act", lambda e, b=b, n0=n0, N=N, dst=dst, bcol=bcol: e.activation(
                        out=R(dst, N, n0), in_=ps[:, b, 0:N], func=AF.Sigmoid, bias=bcol),
                        reads=["bank%d" % b, "pvec"], writes=[key])
            S.op("act", lambda e: e.activation(out=R(RS), in_=R(RA), func=AF.Exp, scale=DV(4 + c)),
                 reads=["A", "dw%d" % c], writes=["SQ"])
            S.op("act", lambda e: e.activation(out=R(RA), in_=R(RA), func=AF.Exp, scale=DV(c)),
                 reads=["A", "dv%d" % c], writes=["A"])
            S.op("act", lambda e: e.activation(out=R(RS), in_=R(RS), func=AF.Sqrt, scale=-1.0, bias=DV(11)),
                 reads=["SQ", "dvc"], writes=["SQ"])

        def L2b(c):
            p2 = c % 2
            RU = 3 + p2
            U, GG = "U%d" % p2, "GG%d" % p2
            S.op("dve", lambda e: e.tensor_tensor(out=R(RI), in0=R(RI), in1=R(RU), op=ALU.mult),
                 reads=["I", U], writes=["I"])
            S.op("dve", lambda e: e.tensor_tensor(out=R(RI), in0=R(RI), in1=R(RS), op=ALU.mult),
                 reads=["I", "SQ"], writes=["I"])
            S.op("dve", lambda e: e.memset(R(RU, 16, TP), 0.0), reads=[U], writes=[U])
            S.op("dve", lambda e: e.tensor_tensor_scan(out=R(RU, TP), data0=R(RA, TP), data1=R(RI, TP), initial=0.0,
                                                       op0=ALU.mult, op1=ALU.add),
                 reads=["A", "I", U], writes=[U])
            S.op("dve", lambda e: e.tensor_tensor(out=R(RU, NS, SC0), in0=R(RA, NS, SC0), in1=smallt[:, c, :],
                                                  op=ALU.mult),
                 reads=["A", "slT", U], writes=[U])
            S.op("dve", lambda e: e.tensor_tensor(out=R(RU, NS, SC0), in0=R(RU, NS, SC0), in1=R(RI, NS, SC0),
                                                  op=ALU.add),
                 reads=["I", U], writes=[U])
            S.op("dve", lambda e: e.tensor_tensor(out=ymT[:, c, :], in0=R(RU), in1=Rb(1 + p2, 0), op=ALU.mult),
                 reads=[U, GG], writes=["ymT%d" % c])
            S.op("pool", lambda e: e.tensor_copy(out=osm[:, c, 0:16], in_=R(RU, NS, SC0)),
                 reads=[U], writes=["osm"])
            S.op("pool", lambda e: e.tensor_copy(out=osm[:, c, 16:17], in_=R(RU, 1, TP - 1)),
                 reads=[U], writes=["osm"])

        for t in range(6):
            if 0 <= t - 2 < 4:
                L2a(t - 2)
            if 0 <= t - 1 < 4:
                L1(t - 1)
            if 0 <= t - 2 < 4:
                L2b(t - 2)
            if 0 <= t < 4:
                L0(t)
                if t == 3:
                    S.dma("pool", ws[:], w_in.rearrange("(ko p) c -> p ko c", p=128)[:, :, 1024:2048], writes=["ws"])
        S.dma("sp", osm_out, osm[:], reads=["osm"])
        O_VT, O_X, O_QK, O_SG, O_KT, O_BL, O_S0, O_KSV, O_SB, O_V16 = 0, 2104, 4208, 6312, 8416, 9504, 11680, 13728, 14496, 15584
        bl = big[:, O_BL:O_BL + NT]
        qkv = big[:, O_QK:O_QK + 2096].bitcast(BF16)
        qT_ = qkv[:, 0:NT]
        kT_ = qkv[:, NT:2 * NT]
        ktok = big[:, O_KT:O_KT + 1088].bitcast(BF16).rearrange("p (i c) -> p i c", c=128)
        vtok16 = big[:, O_VT:O_VT + 2048].bitcast(BF16).rearrange("p (i c) -> p i c", c=256)
        vt16 = big[0:48, O_V16:O_V16 + 128].bitcast(BF16)

        def vt(i, rows):
            return vtok16[0:rows, i, :] if i < 16 else vt16[0:rows, :]

        def vproj(hp, tiles, extra):
            for i in tiles:
                c0, P = TILES[i]
                b = nextbank()
                for k in range(8):
                    S.op("pe", lambda e, k=k, b=b, c0=c0, P=P: e.matmul(
                        ps[0:P, b, 0:256], lhsT=hT[:, k, c0:c0 + P], rhs=ws[:, k, 512 + 256 * hp:768 + 256 * hp],
                        start=(k == 0), stop=(k == 7)),
                        reads=["ws", "hT%d" % k], writes=["bank%d" % b], signal=(k == 7))
                S.op("act", lambda e, b=b, i=i, P=P: e.activation(out=vt(i, P), in_=ps[0:P, b, 0:256], func=AF.Copy),
                     reads=["bank%d" % b], writes=["vtok"] + extra)
        sgv = big[:, O_SG:O_SG + 2096].bitcast(BF16).rearrange("p (h c) -> p h c", c=NT)
        S0s = [big[:, O_S0:O_S0 + 2048].rearrange("p (j v) -> p j v", v=128),
               gb[:].rearrange("p a b -> p (a b)").rearrange("p (j v) -> p j v", v=128)]
        S0keys = [["S0_0"], ["gb0", "gb1"]]
        aT = big[0:16, O_X:O_X + NT]
        Vexp = big[32:48, O_X:O_X + 1024].bitcast(BF16).rearrange("p (j v) -> p j v", v=128)
        S0b = big[:, 15712:15712 + 1024].bitcast(BF16).rearrange("p (j v) -> p j v", v=128)
        ksv = big[0:48, O_KSV:O_KSV + 768]
        SbA = big[:, O_SB:O_SB + 1088].bitcast(BF16).rearrange("p (i c) -> p i c", c=128)
        WB = ws
        WC = wgu[:, 0:4224].rearrange("p (k c) -> p k c", c=528)
        w_in_r = w_in.rearrange("(ko p) c -> p ko c", p=128)
        for n in range(5):
            n0, N = NTL[n]
            b = nextbank()
            proj_fm(WC, "wgu", 512, 16, hT, "hT", n, b)
            S.op("act", lambda e, b=b, n0=n0, N=N: e.activation(out=aT[:, n0:n0 + N], in_=ps[0:16, b, 0:N],
                                                                func=AF.Copy),
                 reads=["bank%d" % b], writes=["aT", "GG0", "UB0"])

        def gproj(hp, extra):
            for hl in range(2):
                h = 2 * hp + hl
                for n in range(5):
                    n0, N = NTL[n]
                    b = nextbank()
                    proj_fm(WC, "wgu", 128 * h, 128, hT, "hT", n, b)
                    S.op("act", lambda e, b=b, hl=hl, n0=n0, N=N: e.activation(
                        out=sgv[:, hl, n0:n0 + N], in_=ps[:, b, 0:N], func=AF.Silu),
                        reads=["bank%d" % b], writes=["sg"] + extra)

        gproj(0, ["U0"])
        load_brow(0, lnb[0], ALPHA)
        vproj(0, range(16), ["XL"])
        S.barrier()
        for hp in range(2):
            for hl in range(2):
                S.dma("sp", S0s[hp][64 * hl:64 * hl + 64, :, :],
                      sg_in.rearrange("j h d v -> h d j v")[2 * hp + hl], writes=S0keys[hp])
        for hp in range(2):
            S0 = S0s[hp]
            S0k = S0keys[hp]
            if hp == 1:
                gproj(1, [])
            for n in range(5):
                n0, N = NTL[n]
                b = nextbank()
                S.op("pe", lambda e, b=b, n0=n0, N=N, hp=hp: e.matmul(
                    ps[:, b, 0:N], lhsT=alw[0:16, 128 * hp:128 * hp + 128], rhs=aT[:, n0:n0 + N],
                    start=True, stop=True), reads=["alw", "aT"], writes=["bank%d" % b])
                S.op("act", lambda e, b=b, N=N, hp=hp: e.activation(out=stmp[:, 0, 0:N], in_=ps[:, b, 0:N],
                                                                    func=AF.Exp, scale=-1.0, bias=DV(8 + hp)),
                     reads=["bank%d" % b, "dnalb"], writes=["stmp0"])
                S.op("act", lambda e, N=N: e.activation(out=stmp[:, 1, 0:N], in_=stmp[:, 0, 0:N], func=AF.Ln,
                                                        bias=DV(11)),
                     reads=["stmp0", "dvc"], writes=["stmp1"])
                if n < 4:
                    for q in range(4):
                        S.op("dve", lambda e, q=q, n0=n0: e.tensor_tensor_scan(
                            out=bl[:, n0 + 128 * q:n0 + 128 * q + 128], data0=ones[:, :],
                            data1=stmp[:, 1, 128 * q:128 * q + 128], initial=0.0, op0=ALU.mult, op1=ALU.add),
                            reads=["stmp1", "ones"], writes=["bl"])
                else:
                    S.op("dve", lambda e: e.tensor_tensor_scan(
                        out=bl[:, 2048:2064], data0=ones[:, 0:16], data1=stmp[:, 1, 0:16], initial=0.0,
                        op0=ALU.mult, op1=ALU.add), reads=["stmp1", "ones"], writes=["bl"])
                    S.op("dve", lambda e: e.tensor_copy(out=bl[:, 2064:2096], in_=stmp[:, 1, 16:48]),
                         reads=["stmp1"], writes=["bl"])
            blc = bl[:, 0:2048].rearrange("p (i c) -> p i c", c=128)[:, :, 127]
            S.op("act", lambda e, blc=blc: e.activation(out=elast[:, 0:16], in_=blc, func=AF.Exp, scale=-1.0 / 16),
                 reads=["bl"], writes=["elast"])
            S.op("act", lambda e: e.activation(out=elast[:, 16:17], in_=bl[:, TP - 1:TP], func=AF.Exp,
                                               scale=-1.0 / 16), reads=["bl"], writes=["elast"])
            for n in range(5):
                n0, N = NTL[n]
                bq = nextbank()
                proj_fm(WB, "ws", 128 * hp, 128, hT, "hT", n, bq)
                bk = nextbank()
                proj_fm(WB, "ws", 256 + 128 * hp, 128, hT, "hT", n, bk)
                S.op("act", lambda e, n0=n0, N=N: e.activation(out=stmp[:, 0, 0:N], in_=bl[:, n0:n0 + N],
                                                               func=AF.Exp, scale=-1.0 / 16),
                     reads=["bl"], writes=["stmp0"])
                S.op("act", lambda e, n0=n0, N=N: e.activation(out=stmp[:, 1, 0:N], in_=bl[:, n0:n0 + N],
                                                               func=AF.Exp, scale=1.0 / 16),
                     reads=["bl"], writes=["stmp1"])
                S.op("dve", lambda e, bq=bq, n0=n0, N=N: e.scalar_tensor_tensor(
                    out=qT_[:, n0:n0 + N], in0=ps[:, bq, 0:N], scalar=0.125, in1=stmp[:, 0, 0:N],
                    op0=ALU.mult, op1=ALU.mult), reads=["bank%d" % bq, "stmp0"], writes=["qT"])
                S.op("dve", lambda e, bk=bk, n0=n0, N=N: e.tensor_tensor(
                    out=kT_[:, n0:n0 + N], in0=ps[:, bk, 0:N], in1=stmp[:, 1, 0:N], op=ALU.mult),
                    reads=["bank%d" % bk, "stmp1"], writes=["kT"])
                if n == 4:
                    S.op("dve", lambda e, bq=bq: e.tensor_scalar(out=qs[:, :], in0=ps[:, bq, 32:48], scalar1=0.125,
                                                                 scalar2=None, op0=ALU.mult),
                         reads=["bank%d" % bq], writes=["qs"])
                    S.op("dve", lambda e: e.tensor_copy(out=ebs[:, :], in_=stmp[:, 0, 32:48]),
                         reads=["stmp0"], writes=["ebs"])
            for g4 in range(5):
                b = nextbank()
                pb = ps[:, b, :].bitcast(BF16)
                ids = list(range(4 * g4, min(4 * g4 + 4, 17)))
                for q, i in enumerate(ids):
                    c0, P = TILES[i]
                    P = 128 if i < 16 else 16
                    S.op("pe", lambda e, q=q, c0=c0, P=P, pb=pb: e.transpose(
                        out=pb[0:P, q * 128:q * 128 + 128], in_=kT_[:, c0:c0 + P], identity=identb[:, :]),
                        reads=["kT", "identb"], writes=["bank%d" % b], signal=(i == ids[-1]))
                for q, i in enumerate(ids):
                    P = 128 if i < 16 else 16
                    S.op("act", lambda e, q=q, i=i, P=P, pb=pb: e.activation(
                        out=ktok[0:P, i, :], in_=pb[0:P, q * 128:q * 128 + 128], func=AF.Copy),
                        reads=["bank%d" % b], writes=["ktok"])
            vproj(hp, [16] if hp == 0 else range(17), [])
            if hp == 0:
                for (cc0, cn, dc0) in ((256, 256, 0), (512, 512, 256)):
                    b = nextbank()
                    for k in range(8):
                        S.op("pe", lambda e, k=k, b=b, cc0=cc0, cn=cn: e.matmul(
                            ps[0:48, b, 0:cn], lhsT=hT[:, k, 2048:2096], rhs=WB[:, k, cc0:cc0 + cn],
                            start=(k == 0), stop=(k == 7)),
                            reads=["ws", "hT%d" % k], writes=["bank%d" % b], signal=(k == 7))
                    S.op("act", lambda e, b=b, cn=cn, dc0=dc0: e.activation(out=ksv[:, dc0:dc0 + cn], in_=ps[0:48, b, 0:cn],
                                                                            func=AF.Copy),
                         reads=["bank%d" % b], writes=["ksv"])
                    if dc0 == 0:
                        S.op("act", lambda e, b=b: e.activation(out=ksb[:, :], in_=ps[0:48, b, 0:256], func=AF.Copy),
                             reads=["bank%d" % b], writes=["ksb"])
            if hp == 1:
                S.dma("pool", ws[:], w_out.rearrange("(ko p) c -> p ko c", p=128), writes=["ws"])
            SB = [0, 1, 2, 3]
            for hl in range(2):
                h = 2 * hp + hl
                S.op("dve", lambda e, h=h: e.tensor_tensor(
                    out=Vexp, in0=ksv[32:48, 256 + 128 * h:384 + 128 * h].unsqueeze(1).to_broadcast([16, 16, 128]),
                    in1=ident[32:48, 32:48].unsqueeze(2).to_broadcast([16, 16, 128]), op=ALU.mult),
                    reads=["ksv", "ident"], writes=["Vexp"])
                for n4 in range(4):
                    S.op("pe", lambda e, n4=n4, h=h, hl=hl: e.matmul(
                        ps[64 * hl:64 * hl + 64, SB[n4], :], lhsT=ksb[32:48, 64 * h:64 * h + 64],
                        rhs=Vexp[:, 4 * n4:4 * n4 + 4, :].rearrange("p j v -> p (j v)"), start=True, stop=True),
                        reads=["ksb", "Vexp"], writes=["bank%d" % SB[n4]])
            S.op("dve", lambda e: e.memset(S32[:], 0.0), reads=["S32"], writes=["S32"])
            SCB = [4, 5]
            RB = 6
            PBK = 7

            def emit_P(g4):
                ids = list(range(4 * g4, min(4 * g4 + 4, 17)))
                for q, i in enumerate(ids):
                    C = 128 if i < 16 else 16
                    for hl in range(2):
                        S.op("pe", lambda e, hl=hl, i=i, C=C, q=q: e.matmul(
                            ps[64 * hl:64 * hl + 64, PBK, 128 * q:128 * q + 128],
                            lhsT=ktok[0:C, i, 64 * hl:64 * hl + 64],
                            rhs=vt(i, C)[:, 128 * hl:128 * hl + 128], start=True, stop=True),
                            reads=["ktok", "vtok"], writes=["bank%d" % PBK],
                            signal=(i == ids[-1] and hl == 1))

            def emit_Schain(i):
                q = i % 4
                S.op("dve", lambda e: e.tensor_tensor(out=Stmp[:, :], in0=S32[:, :],
                                                      in1=ps[:, PBK, 128 * q:128 * q + 128], op=ALU.add),
                     reads=["S32", "bank%d" % PBK], writes=["Stmp", "bank%d" % PBK])
                S.op("dve", lambda e: e.tensor_scalar(out=S32[:, :], in0=Stmp[:, :],
                                                      scalar1=elast[:, i:i + 1], scalar2=None, op0=ALU.mult),
                     reads=["Stmp", "elast"], writes=["S32"])
                if i < 16:
                    S.op("act", lambda e: e.activation(out=SbA[:, i + 1, :], in_=Stmp[:, :], func=AF.Identity,
                                                       scale=elast[:, i:i + 1]),
                         reads=["Stmp", "elast"], writes=["SbA%d" % (i + 1)])

            def OBof(i):
                return [0, 1] if (i // 4) % 2 == 0 else [2, 3]

            def sc_op(qq):
                i, hl = qq // 2, qq % 2
                c0 = 128 * i
                C = 128 if i < 16 else 16
                bs = SCB[qq % 2]
                sp2 = i % 2
                S.op("pe", lambda e: e.matmul(
                    ps[0:C, bs, 0:C], lhsT=kT_[64 * hl:64 * hl + 64, c0:c0 + C],
                    rhs=qT_[64 * hl:64 * hl + 64, c0:c0 + C], start=True, stop=True),
                    reads=["kT", "qT"], writes=["bank%d" % bs])
                S.op("dve", lambda e: e.tensor_tensor(
                    out=scm[0:C, 2 * sp2 + hl, 0:C], in0=ps[0:C, bs, 0:C], in1=tri[0:C, 0:C], op=ALU.mult),
                    reads=["bank%d" % bs, "tri"], writes=["scm%d_%d" % (sp2, hl), "bank%d" % bs])

            def o_op(qq):
                i, hl = qq // 2, qq % 2
                c0 = 128 * i
                C = 128 if i < 16 else 16
                gcol = (i % 4) * 128
                OB = OBof(i)
                sp2 = i % 2
                S.op("pe", lambda e: e.matmul(
                    ps[:, OB[hl], gcol:gcol + C], lhsT=vt(i, C)[:, 128 * hl:128 * hl + 128],
                    rhs=scm[0:C, 2 * sp2 + hl, 0:C], start=True, stop=(i == 0)),
                    reads=["vtok", "scm%d_%d" % (sp2, hl)], writes=["bank%d" % OB[hl]], signal=(i == 0))
                if i > 0:
                    S.op("pe", lambda e: e.matmul(
                        ps[:, OB[hl], gcol:gcol + C], lhsT=SbA[64 * hl:64 * hl + 64, i, :],
                        rhs=qT_[64 * hl:64 * hl + 64, c0:c0 + C], start=False, stop=True),
                        reads=["SbA%d" % i, "qT"], writes=["bank%d" % OB[hl]])

            sqb = [xin[:, 0, 0:512], xin[:, 1, 0:512]]
            rsb = [xin[:, 0, 512:1024], xin[:, 1, 512:1024]]
            tbf = [ht[:, 0, 0:512], ht[:, 1, 0:512]]

            def rms_a(g):
                n0, N = NTL[g]
                OB = OBof(4 * g)
                for hl in range(2):
                    S.op("act", lambda e, hl=hl: e.activation(out=sqb[hl][:, 0:N], in_=ps[:, OB[hl], 0:N],
                                                              func=AF.Square),
                         reads=["bank%d" % OB[hl]], writes=["sqb%d" % hl])

            def rms_b1(g):
                n0, N = NTL[g]
                for hl in range(2):
                    S.op("pe", lambda e, hl=hl: e.matmul(ps[:, RB, 0:N], lhsT=ones[:, :], rhs=sqb[hl][:, 0:N],
                                                         start=True, stop=True),
                         reads=["ones", "sqb%d" % hl], writes=["bank%d" % RB])
                    S.op("act", lambda e, hl=hl: e.activation(out=rsb[hl][:, 0:N], in_=ps[:, RB, 0:N], func=AF.Ln,
                                                              scale=1.0 / 128, bias=DV(12)),
                         reads=["bank%d" % RB, "dvc"], writes=["rsb%d" % hl])
                    S.op("act", lambda e, hl=hl: e.activation(out=rsb[hl][:, 0:N], in_=rsb[hl][:, 0:N], func=AF.Exp,
                                                              scale=-0.5),
                         reads=["rsb%d" % hl], writes=["rsb%d" % hl])

            def rms_b2(g):
                n0, N = NTL[g]
                OB = OBof(4 * g)
                for hl in range(2):
                    h = 2 * hp + hl
                    ob = "bank%d" % OB[hl]
                    S.op("dve", lambda e, hl=hl, h=h: e.scalar_tensor_tensor(
                        out=tbf[hl][:, 0:N], in0=ps[:, OB[hl], 0:N], scalar=PV_GNG(h), in1=rsb[hl][:, 0:N],
                        op0=ALU.mult, op1=ALU.mult), reads=[ob, "rsb%d" % hl, "pvec"], writes=["tbf%d" % hl, ob])
                    S.op("pool", lambda e, hl=hl, h=h: e.tensor_tensor(
                        out=ymT[:, 4 + h, n0:n0 + N], in0=tbf[hl][:, 0:N], in1=sgv[:, hl, n0:n0 + N],
                        op=ALU.mult), reads=["tbf%d" % hl, "sg"], writes=["ymT%d" % (4 + h)])

            emit_P(0)
            sc_op(0)
            for n4 in range(4):
                S0v = S0[:, 4 * n4:4 * n4 + 4, :]
                S.op("dve", lambda e, n4=n4, S0v=S0v: e.tensor_tensor(
                    out=S0v, in0=S0v, in1=ebs[:, 4 * n4:4 * n4 + 4].unsqueeze(2).to_broadcast([128, 4, 128]),
                    op=ALU.mult), reads=S0k + ["ebs"], writes=S0k)
                S.op("dve", lambda e, n4=n4, S0v=S0v: e.tensor_tensor(
                    out=S0v, in0=S0v, in1=ps[:, SB[n4], :].rearrange("p (j v) -> p j v", v=128), op=ALU.add),
                    reads=S0k + ["bank%d" % SB[n4]], writes=S0k)
            for hl in range(2):
                S.dma("sp", gla_s.rearrange("j h d v -> h d j v")[2 * hp + hl], S0[64 * hl:64 * hl + 64, :, :],
                      reads=S0k)
            S.op("act", lambda e, S0=S0: e.activation(out=S0b, in_=S0, func=AF.Copy), reads=S0k, writes=["S0b"])
            pending = None
            pend2 = None
            for qq in range(34):
                i, hl = qq // 2, qq % 2
                if hl == 0:
                    emit_Schain(i)
                    if i % 4 == 3 and i < 16:
                        emit_P(i // 4 + 1)
                    if i == 16:
                        for h2 in range(2):
                            S.op("dve", lambda e, h2=h2: e.memset(ps[:, OBof(16)[h2], 0:48], 0.0),
                                 reads=["bank%d" % OBof(16)[h2]], writes=["bank%d" % OBof(16)[h2]])
                if qq + 1 < 34:
                    sc_op(qq + 1)
                o_op(qq)
                if hl == 1:
                    if pend2 is not None:
                        rms_b2(pend2)
                        pend2 = None
                    if pending is not None:
                        rms_b1(pending)
                        pend2 = pending
                        pending = None
                    if i % 4 == 3:
                        rms_a(i // 4)
                        pending = i // 4
            if pend2 is not None:
                rms_b2(pend2)
            if pending is not None:
                rms_b1(pending)
                rms_b2(pending)
            for hl in range(2):
                for j in range(NS):
                    S.op("pe", lambda e, hl=hl, j=j, S0=S0: e.matmul(
                        ps[:, OBof(16)[hl], 32 + j:33 + j], lhsT=S0b[64 * hl:64 * hl + 64, j, :],
                        rhs=qs[64 * hl:64 * hl + 64, j:j + 1], start=True, stop=True),
                        reads=["S0b", "qs"], writes=["bank%d" % OBof(16)[hl]], signal=(j == NS - 1))
            rms_a(4)
            rms_b1(4)
            rms_b2(4)
            S.dma("sp", gla_p.rearrange("h d v -> (h d) v")[128 * hp:128 * hp + 128, :], S32[:, :], reads=["S32"])

        load_gb(0, lng[0], ALPHA)
        load_gb(1, lng[1], ALPHA)
        S.barrier()
        wgu_v = wgu[:, 0:4096].rearrange("p (s g k c) -> p s g k c", s=2, g=2, k=8)
        wg_r = wg.rearrange("(ko p) f -> p ko f", p=128)
        wu_r = wu.rearrange("(ko p) f -> p ko f", p=128)
        flist = [f0 + fl for (f0, nf) in FPASS for fl in range(nf)]
        wgu_emitted = [0]

        def emit_wgu_upto(idx):
            while wgu_emitted[0] <= min(idx, NF - 1):
                j = wgu_emitted[0]
                f = flist[j]
                S.dma("pool", wgu_v[:, j % 2, 0, :, :], wg_r[:, :, 128 * f:128 * f + 128], writes=["wgu%d" % (j % 2)])
                S.dma("pool", wgu_v[:, j % 2, 1, :, :], wu_r[:, :, 128 * f:128 * f + 128], writes=["wgu%d" % (j % 2)])
                wgu_emitted[0] += 1

        emit_wgu_upto(1)
        ymk = ["ymT%d" % k for k in range(8)]
        pair_ctr = [0]

        def nextpair():
            b = (pair_ctr[0] % 4) * 2
            pair_ctr[0] += 1
            return b

        def D0_front(i):
            c0, P = TILES[i]
            q = i % 4
            xb, xk = xbuf[q], "xb%d" % q
            load_x_buf(i)
            mb0 = 2 * (i % 2)
            mb = [mb0, mb0 + 1]
            for half in range(2):
                S.op("pe", lambda e, half=half: e.matmul(
                    ps[0:P, mb[half], :], lhsT=onesb[0:33, 0:P], rhs=hlb[0:33, 0, 512 * half:512 * half + 512],
                    start=True, stop=False), reads=["onesb", "brow0"], writes=["bank%d" % mb[half]], signal=False)
                for k in range(8):
                    S.op("pe", lambda e, k=k, half=half: e.matmul(
                        ps[0:P, mb[half], :], lhsT=ymT[:, k, c0:c0 + P], rhs=ws[:, k, 512 * half:512 * half + 512],
                        start=False, stop=(k == 7)),
                        reads=["ws", ymk[k]], writes=["bank%d" % mb[half]], signal=(k == 7))
            S.op("act", lambda e: e.activation(
                out=xb[:P, :], in_=xb[:P, :], func=AF.Identity,
                scale=stat0[:P, i, 0:1], bias=stat0[:P, i, 1:2]),
                reads=[xk, "st0_%dr" % i, "st0_%dn" % i], writes=[xk])

        def D0_mult(i):
            c0, P = TILES[i]
            q = i % 4
            xb, xk = xbuf[q], "xb%d" % q
            S.op("dve", lambda e: e.tensor_tensor(out=xb[:P, :], in0=xb[:P, :], in1=gb[:P, 0, :], op=ALU.mult),
                 reads=[xk, "gb0"], writes=[xk])

        def D1a(i):
            c0, P = TILES[i]
            q, par = i % 4, i % 2
            xb, xk = xbuf[q], "xb%d" % q
            mb0 = 2 * par
            S.op("dve", lambda e: e.tensor_tensor(
                out=xb[:P, :].rearrange("p (a c) -> p a c", c=512),
                in0=xb[:P, :].rearrange("p (a c) -> p a c", c=512),
                in1=ps[0:P, mb0:mb0 + 2, :], op=ALU.add),
                reads=[xk, "bank%d" % mb0, "bank%d" % (mb0 + 1)], writes=[xk])
            kb = "bst%d" % par
            S.op("dve", lambda e: e.bn_stats(out=bst[:P, par, 0, :], in_=xb[:P, 0:512]), reads=[xk], writes=[kb + "a"])
            S.op("dve", lambda e: e.bn_stats(out=bst[:P, par, 1, :], in_=xb[:P, 512:1024]), reads=[xk],
                 writes=[kb + "b"])
            S.op("dve", lambda e: e.bn_aggr(out=mv[:P, par, :], in_=bst[:P, par, :, :].rearrange("p a b -> p (a b)")),
                 reads=[kb + "a", kb + "b"], writes=["mv%d" % par])

        def D1b_act(i):
            c0, P = TILES[i]
            par = i % 2
            S.op("act", lambda e: e.activation(out=sd[:P, par, 0:1], in_=mv[:P, par, 1:2], func=AF.Sqrt,
                                               bias=dvec[:P, 10:11]),
                 reads=["mv%d" % par, "dvc"], writes=["sd%d" % par])

        def D1b_dve(i):
            c0, P = TILES[i]
            par = i % 2
            key = "st1_%d" % par
            S.op("dve", lambda e: e.reciprocal(out=rn[:P, par, 0:1], in_=sd[:P, par, 0:1]),
                 reads=["sd%d" % par], writes=[key + "r"])
            S.op("dve", lambda e: e.tensor_scalar(out=rn[:P, par, 1:2], in0=mv[:P, par, 0:1],
                                                  scalar1=rn[:P, par, 0:1], scalar2=-1.0, op0=ALU.mult, op1=ALU.mult),
                 reads=["mv%d" % par, key + "r"], writes=[key + "n"])

        def D1c(i):
            c0, P = TILES[i]
            q, par = i % 4, i % 2
            xb, xk = xbuf[q], "xb%d" % q
            hi = big[:P, i * 1024:(i + 1) * 1024]
            S.op("act", lambda e: e.activation(
                out=hi, in_=xb[:P, :], func=AF.Identity, scale=rn[:P, par, 0:1], bias=rn[:P, par, 1:2]),
                reads=[xk, "st1_%dr" % par, "st1_%dn" % par], writes=["big%d" % i])

        def D2(i):
            c0, P = TILES[i]
            hi = big[:P, i * 1024:(i + 1) * 1024]
            transpose_tile(hi, P, c0, hT, ["big%d" % i], "hT", 56, tbank)

        for s_ in range(17 + 4):
            for fn, sk in ((D1c, 3), (D1b_act, 2), (D0_front, 0), (D1a, 1), (D1b_dve, 2), (D0_mult, 0), (D2, 4)):
                if 0 <= s_ - sk < 17:
                    fn(s_ - sk)

        wd_r = wd.rearrange("(f p) d -> p f d", p=128)
        load_gb(0, lng[2])
        load_brow(1, lnb[1], ALPHA)
        emit_wgu_upto(1)
        S.dma("pool", ws[:, 0:FPASS[0][1], :], wd_r[:, 0:FPASS[0][1], :], writes=["ws"])
        fcount = 0
        for pi, (f0, nf) in enumerate(FPASS):
            if pi == 1:
                load_gb(1, lnb[2])
            for fl in range(nf):
                f = f0 + fl
                slot = fcount % 2
                emit_wgu_upto(fcount + 1)
                fcount += 1
                wk = "wgu%d" % slot
                for n in range(5):
                    n0, N = NTL[n]
                    bg, bu = nextbank(), nextbank()
                    for (g, b) in ((0, bg), (1, bu)):
                        for k in range(8):
                            S.op("pe", lambda e, k=k, g=g, b=b, n0=n0, N=N, slot=slot: e.matmul(
                                ps[:, b, 0:N], lhsT=wgu_v[:, slot, g, k, :], rhs=hT[:, k, n0:n0 + N],
                                start=(k == 0), stop=(k == 7)),
                                reads=[wk, "hT%d" % k], writes=["bank%d" % b], signal=(k == 7))
                    sp_ = n % 2
                    S.op("act", lambda e, bg=bg, N=N, sp_=sp_: e.activation(out=stmp[:, sp_, 0:N], in_=ps[:, bg, 0:N],
                                                                           func=AF.Silu),
                         reads=["bank%d" % bg], writes=["stmp%d" % sp_])
                    S.op("dve", lambda e, bu=bu, n0=n0, N=N, fl=fl, sp_=sp_: e.tensor_tensor(
                        out=ymT[:, fl, n0:n0 + N], in0=stmp[:, sp_, 0:N], in1=ps[:, bu, 0:N], op=ALU.mult),
                        reads=["stmp%d" % sp_, "bank%d" % bu], writes=["ymT%d" % fl])
            last = (f0 + nf == NF)
            first = (f0 == 0)
            emit_wgu_upto(fcount + 1)

            def T0(i, nf=nf, first=first, last=last):
                c0, P = TILES[i]
                par = i % 2
                hi = big[:P, i * 1024:(i + 1) * 1024]
                bk = "big%d" % i
                mb0 = nextpair()
                for half in range(2):
                    b = mb0 + half
                    if first:
                        S.op("pe", lambda e, half=half, b=b: e.matmul(
                            ps[0:P, b, :], lhsT=onesb[0:33, 0:P], rhs=hlb[0:33, 1, 512 * half:512 * half + 512],
                            start=True, stop=False), reads=["onesb", "brow1"], writes=["bank%d" % b], signal=False)
                    for fl in range(nf):
                        S.op("pe", lambda e, fl=fl, half=half, b=b: e.matmul(
                            ps[0:P, b, :], lhsT=ymT[:, fl, c0:c0 + P], rhs=ws[:, fl, 512 * half:512 * half + 512],
                            start=(fl == 0 and not first), stop=(fl == nf - 1)),
                            reads=["ws", "ymT%d" % fl], writes=["bank%d" % b], signal=(fl == nf - 1))
                if first:
                    S.op("dve", lambda e: e.tensor_tensor(out=hi, in0=hi, in1=gb[:P, 1, :], op=ALU.mult),
                         reads=[bk, "gb1"], writes=[bk])
                if not last:
                    S.op("dve", lambda e: e.tensor_tensor(
                        out=hi.rearrange("p (a c) -> p a c", c=512), in0=hi.rearrange("p (a c) -> p a c", c=512),
                        in1=ps[0:P, mb0:mb0 + 2, :], op=ALU.add),
                        reads=[bk, "bank%d" % mb0, "bank%d" % (mb0 + 1)], writes=[bk])
                else:
                    S.op("dve", lambda e: e.scalar_tensor_tensor(
                        out=hi.rearrange("p (a c) -> p a c", c=512), in0=ps[0:P, mb0:mb0 + 2, :], scalar=1.0,
                        in1=hi.rearrange("p (a c) -> p a c", c=512), op0=ALU.mult, op1=ALU.add,
                        accum_out=ssum[:P, par, 0:1]),
                        reads=[bk, "bank%d" % mb0, "bank%d" % (mb0 + 1)], writes=[bk, "ssum%d_0" % par])
                    S.op("act", lambda e: e.activation(out=xin[:P, par, :], in_=hi, func=AF.Square,
                                                       accum_out=ssum[:P, par, 1:2]),
                         reads=[bk], writes=["xinj%d" % par, "ssum%d_1" % par])

            def T1(i):
                c0, P = TILES[i]
                par = i % 2
                key = "st2_%d" % par
                S.op("dve", lambda e: e.tensor_scalar(out=mv[:P, par, 0:1], in0=ssum[:P, par, 0:1], scalar1=1.0 / D,
                                                      scalar2=None, op0=ALU.mult),
                     reads=["ssum%d_0" % par], writes=["mv%d" % par])
                S.op("dve", lambda e: e.tensor_tensor(out=ssum[:P, par, 2:3], in0=mv[:P, par, 0:1],
                                                      in1=mv[:P, par, 0:1], op=ALU.mult),
                     reads=["mv%d" % par], writes=["ssum%d_2" % par])
                S.op("dve", lambda e: e.scalar_tensor_tensor(out=mv[:P, par, 1:2], in0=ssum[:P, par, 1:2],
                                                             scalar=1.0 / D, in1=ssum[:P, par, 2:3],
                                                             op0=ALU.mult, op1=ALU.subtract),
                     reads=["ssum%d_1" % par, "ssum%d_2" % par, "mv%d" % par], writes=["mv%d" % par])
                S.op("act", lambda e: e.activation(out=sd[:P, par, 0:1], in_=mv[:P, par, 1:2], func=AF.Sqrt,
                                                   bias=dvec[:P, 10:11]),
                     reads=["mv%d" % par, "dvc"], writes=["sd%d" % par])
                S.op("dve", lambda e: e.reciprocal(out=rn[:P, par, 0:1], in_=sd[:P, par, 0:1]),
                     reads=["sd%d" % par], writes=[key + "r"])
                S.op("dve", lambda e: e.tensor_scalar(out=rn[:P, par, 1:2], in0=mv[:P, par, 0:1],
                                                      scalar1=rn[:P, par, 0:1], scalar2=-1.0,
                                                      op0=ALU.mult, op1=ALU.mult),
                     reads=["mv%d" % par, key + "r"], writes=[key + "n"])

            def T2(i):
                c0, P = TILES[i]
                par = i % 2
                hi = big[:P, i * 1024:(i + 1) * 1024]
                S.op("act", lambda e: e.activation(
                    out=ht[:P, par, :], in_=hi, func=AF.Identity, scale=rn[:P, par, 0:1], bias=rn[:P, par, 1:2]),
                    reads=["big%d" % i, "st2_%dr" % par, "st2_%dn" % par], writes=["ht%d" % par])

            def T3(i):
                c0, P = TILES[i]
                par = i % 2
                hk = "ht%d" % par
                S.op("dve", lambda e: e.tensor_tensor(out=ht[:P, par, :], in0=ht[:P, par, :],
                                                      in1=gb[:P, 0, :], op=ALU.mult),
                     reads=[hk, "gb0"], writes=[hk])
                S.op("dve", lambda e: e.tensor_tensor(out=ht[:P, par, :], in0=ht[:P, par, :],
                                                      in1=gb[:P, 1, :], op=ALU.add),
                     reads=[hk, "gb1"], writes=[hk])
                if i == 0:
                    S.dma("sp", y_p[0:112, :], ht[16:128, par, :], reads=[hk])
                elif i < 16:
                    S.dma("sp", y_p[128 * i - 16:128 * i + 112, :], ht[:, par, :], reads=[hk])
                else:
                    S.dma("sp", y_p[2032:2048, :], ht[0:16, par, :], reads=[hk])
                    S.dma("sp", y_s, ht[32:48, par, :], reads=[hk])

            if last:
                pipeline([T0, T1, T2, T3], 17)
            else:
                for i in range(17):
                    T0(i)
            if pi + 1 < len(FPASS):
                nf0, nnf = FPASS[pi + 1]
                S.dma("pool", ws[:, 0:nnf, :], wd_r[:, nf0:nf0 + nnf, :], writes=["ws"])

        S.finish()
        with nc.Block() as block:
            S.emit(block, sems)
    return nc


_NC_CACHE = {}


def kernel(x_prompt, x_sample, state_gla, state_lru, state_conv, meta_tokens, ln_in_g, ln_in_b,
           w_in, conv_w, conv_b, lru_gate_a_w, lru_gate_a_b, lru_gate_x_w, lru_gate_x_b, lru_lambda,
           gla_alpha_w, gla_alpha_b, gla_norm_g, w_out, ln1_g, ln1_b, w_ffn_gate, w_ffn_up, w_ffn_down,
           ln2_g, ln2_b):
    f = lambda a: np.ascontiguousarray(np.asarray(a, dtype=np.float32))
    n = 8
    pvec = np.zeros((128, NPV), np.float32)
    cw = f(conv_w)[0]
    for c in range(4):
        sl = slice(128 * c, 128 * c + 128)
        for j in range(4):
            pvec[:, c * 8 + j] = cw[j, sl]
        pvec[:, c * 8 + 4] = f(conv_b)[0, sl]
        pvec[:, c * 8 + 5] = f(lru_gate_a_b)[0, sl]
        pvec[:, c * 8 + 6] = f(lru_gate_x_b)[0, sl]
        pvec[:, c * 8 + 7] = f(lru_lambda)[0, sl]
    for hp in range(2):
        pvec[:, 32 + hp] = f(gla_alpha_b)[0, 128 * hp:128 * hp + 128]
    for h in range(4):
        pvec[:, 34 + h] = f(gla_norm_g)[0, 128 * h:128 * h + 128]
    for k in range(8):
        sl = slice(128 * k, 128 * k + 128)
        pvec[:, 40 + k] = f(ln_in_g)[sl]
        pvec[:, 48 + k] = f(ln_in_b)[sl]
        pvec[:, 56 + k] = f(ln1_g)[0, sl]
        pvec[:, 64 + k] = f(ln1_b)[0, sl]
    ident = np.eye(128, dtype=np.float32)
    tri = np.triu(np.ones((128, 128), np.float32))
    shared = {
        "meta": f(meta_tokens), "ln_in_g": f(ln_in_g), "ln_in_b": f(ln_in_b),
        "ln1_g": f(ln1_g)[0], "ln1_b": f(ln1_b)[0], "ln2_g": f(ln2_g)[0], "ln2_b": f(ln2_b)[0],
        "w_in": f(w_in)[0], "gaw": f(lru_gate_a_w)[0], "gxw": f(lru_gate_x_w)[0], "alw": f(gla_alpha_w)[0],
        "pvec": pvec, "w_out": f(w_out)[0], "wg": f(w_ffn_gate)[0], "wu": f(w_ffn_up)[0], "wd": f(w_ffn_down)[0],
        "ident": ident, "tri": tri,
    }
    xpr, xsa = f(x_prompt), f(x_sample)
    sgl, slr, scv = f(state_gla)[0], f(state_lru)[0], f(state_conv)[0]
    in_maps = []
    for c in range(n):
        js = slice(NS * c, NS * c + NS)
        m = dict(shared)
        m["xp"] = xpr[c]
        m["xs"] = np.ascontiguousarray(xsa[js, 0, :])
        m["sg"] = np.ascontiguousarray(sgl[js])
        m["slT"] = np.ascontiguousarray(slr[js].reshape(NS, 4, 128).transpose(2, 1, 0))
        m["scT"] = np.ascontiguousarray(scv[js].reshape(NS, 3, 4, 128).transpose(3, 2, 1, 0))
        in_maps.append(m)
    if "nc" not in _NC_CACHE:
        _NC_CACHE["nc"] = build_nc()
    res = run_bass_kernel_spmd(_NC_CACHE["nc"], in_maps, core_ids=list(range(n)))
    R = res.results
    y_prompt = np.stack([R[c]["y_p"] for c in range(n)], 0)
    y_sample = np.concatenate([R[c]["y_s"] for c in range(n)], 0)[:, None, :]
    gla_prompt = np.stack([R[c]["gla_p"] for c in range(n)], 0)[None]
    lru_prompt = np.stack([R[c]["osm"][:, :, 16].T.reshape(512) for c in range(n)], 0)[None]
    conv_prompt = np.stack([R[c]["osm"][:, :, 17:20].transpose(2, 1, 0).reshape(3, 512) for c in range(n)], 0)[None]
    gla_sample = np.concatenate([R[c]["gla_s"] for c in range(n)], 0)[None]
    lru_sample = np.concatenate([R[c]["osm"][:, :, 0:16].transpose(2, 1, 0).reshape(NS, 512) for c in range(n)], 0)[None]
    conv_sample = np.concatenate([R[c]["conv_s"].transpose(3, 2, 1, 0).reshape(NS, 3, 512) for c in range(n)], 0)[None]
    outs = (y_prompt, y_sample, gla_prompt, lru_prompt, conv_prompt, gla_sample, lru_sample, conv_sample)
    return tuple(np.ascontiguousarray(o, dtype=np.float32) for o in outs)
```
